# Optimizing a Trainium2 kernel written in Bass

```python
import jax, jax.numpy as jnp
from jax import lax
import numpy as np

D_MODEL = 1024
BATCH = 16
SEQ = 2048
DEPTH = 4

GRID_W = 64
CTX_LEN = 256
D_MIX = 1024
HEAD_DIM = 64
EPS = 1e-6
A_GROUPS = 4
A_WIDTH = A_GROUPS * HEAD_DIM
CHUNK = 128
B_HEADS = 8
B_KV_HEADS = 2
B_GROUP = B_HEADS // B_KV_HEADS
B_WIDTH = B_HEADS * HEAD_DIM
B_KV_WIDTH = B_KV_HEADS * HEAD_DIM
Q_BLOCK = 128
ATTN_SCALE = HEAD_DIM ** -0.5
ROPE_THETA = 10000.0
HALF_ROT = HEAD_DIM // 2
ROPE_FREQS = HALF_ROT // 2
C_HEADS = 4
C_WIDTH = C_HEADS * HEAD_DIM
M_CHUNK = 128
F_BIAS_LO = 3.0
F_BIAS_HI = 6.0
SPLIT_SIZES = (A_WIDTH, A_WIDTH, A_WIDTH,
               B_WIDTH, B_KV_WIDTH, B_KV_WIDTH, B_WIDTH,
               C_WIDTH, C_WIDTH, C_WIDTH, C_WIDTH, C_WIDTH, 4 * C_HEADS)
D_IN = 3 * A_WIDTH + 2 * B_WIDTH + 2 * B_KV_WIDTH + 5 * C_WIDTH + 4 * C_HEADS

kernel_name = 'hybrid_gmlp_gqa_mlstm_dit_block'


def _rms_norm(x, g):
    xf = x.astype(jnp.float32)
    y = xf * lax.rsqrt(jnp.mean(xf * xf, axis=-1, keepdims=True) + EPS)
    return (y * g.astype(jnp.float32)).astype(x.dtype)


def _layer_norm(x):
    xf = x.astype(jnp.float32)
    mu = jnp.mean(xf, axis=-1, keepdims=True)
    var = jnp.mean(jnp.square(xf - mu), axis=-1, keepdims=True)
    return ((xf - mu) * lax.rsqrt(var + EPS)).astype(x.dtype)


def _split_proj(p):
    idx = [int(i) for i in np.cumsum(SPLIT_SIZES)[:-1]]
    return jnp.split(p, idx, axis=-1)


def _rope_tables(rows_n):
    row = jnp.repeat(jnp.arange(rows_n, dtype=jnp.int32), GRID_W)
    col = jnp.tile(jnp.arange(GRID_W, dtype=jnp.int32), rows_n)
    freqs = ROPE_THETA ** (-jnp.arange(ROPE_FREQS, dtype=jnp.float32) / ROPE_FREQS)
    ang_r = row.astype(jnp.float32)[:, None] * freqs
    ang_c = col.astype(jnp.float32)[:, None] * freqs
    return (jnp.cos(ang_r), jnp.sin(ang_r), jnp.cos(ang_c), jnp.sin(ang_c))


def _apply_rope(x, tabs):
    cos_r, sin_r, cos_c, sin_c = tabs
    shp = (1, x.shape[1]) + (1,) * (x.ndim - 3) + (ROPE_FREQS,)
    xf = x.astype(jnp.float32)

    def rot(xh, cos, sin):
        cos = cos.reshape(shp)
        sin = sin.reshape(shp)
        x1, x2 = xh[..., :ROPE_FREQS], xh[..., ROPE_FREQS:]
        return jnp.concatenate([x1 * cos - x2 * sin, x2 * cos + x1 * sin], axis=-1)

    out = jnp.concatenate([rot(xf[..., :HALF_ROT], cos_r, sin_r),
                           rot(xf[..., HALF_ROT:], cos_c, sin_c)], axis=-1)
    return out.astype(x.dtype)


def _attend(q, k, v):
    s = jnp.einsum('bqkgd,bskd->bkgqs', q, k).astype(jnp.float32) * ATTN_SCALE
    p = jax.nn.softmax(s, axis=-1).astype(v.dtype)
    return jnp.einsum('bkgqs,bskd->bqkgd', p, v)


def _gmlp_branch(u, v, z, w_s, b_s):
    B_, T, _ = u.shape
    shp = (B_, T // CHUNK, CHUNK, A_GROUPS, HEAD_DIM)
    vn = _layer_norm(jax.nn.gelu(v).reshape(shp))
    sv = jnp.einsum('gpq,bcqgd->bcpgd', w_s, vn) + b_s.T[:, :, None]
    return (jax.nn.gelu(u).reshape(shp) * sv).reshape(B_, T, A_WIDTH) * jax.nn.silu(z)


def _mlstm_scan(q, k, v, log_i, log_f, state, emit_h):
    B_, T, H, D = q.shape
    nc = T // M_CHUNK

    def chunks(a):
        return a.reshape((B_, nc, M_CHUNK) + a.shape[2:]).swapaxes(0, 1)

    xs = (chunks(q), chunks(k), chunks(v), chunks(log_i), chunks(log_f))
    lower = jnp.tril(jnp.ones((M_CHUNK, M_CHUNK), dtype=bool))[None, :, :, None]

    def step(carry, inp):
        C, n, m = carry
        qc, kc, vc, li, lf = inp
        qf = qc.astype(jnp.float32)
        kf = kc.astype(jnp.float32) * (D ** -0.5)
        vf = vc.astype(jnp.float32)
        b = jnp.cumsum(lf, axis=1)
        b_end = b[:, -1]
        w_end = b_end[:, None] - b + li
        m_new = jnp.maximum(b_end + m, jnp.max(w_end, axis=1))
        decay = jnp.exp(b_end + m - m_new)
        wk = jnp.exp(w_end - m_new[:, None])
        C_new = decay[..., None, None] * C + jnp.einsum('blh,blhd,blhe->bhde', wk, vf, kf)
        n_new = decay[..., None] * n + jnp.einsum('blh,blhe->bhe', wk, kf)
        h = None
        if emit_h:
            a = b + m[:, None]
            dmat = b[:, :, None] - b[:, None] + li[:, None]
            dmat = jnp.where(lower, dmat, -jnp.inf)
            m_t = jnp.maximum(a, jnp.max(dmat, axis=2))
            inter = jnp.exp(a - m_t)
            s = jnp.einsum('bthd,bshd->btsh', qf, kf) * jnp.exp(dmat - m_t[:, :, None])
            num = jnp.einsum('btsh,bshd->bthd', s, vf) + inter[..., None] * jnp.einsum('bhde,bthe->bthd', C, qf)
            den = jnp.sum(s, axis=2) + inter * jnp.einsum('bhe,bthe->bth', n, qf)
            h = num / jnp.maximum(jnp.abs(den), jnp.exp(-m_t))[..., None]
        return (C_new, n_new, m_new), h

    state, hs = lax.scan(step, state, xs)
    h = hs.swapaxes(0, 1).reshape(B_, T, H, D).astype(q.dtype) if emit_h else None
    return h, state


def _layer(x, ctx, mod_lat, mod_ctx, tabs, g_norm, w_in, w_s, b_s, g_q, g_k, b_gates, g_head, w_out,
           with_ctx_out):
    B_, S, _ = x.shape
    Lc = ctx.shape[1]
    sh, sc, gt = jnp.split(mod_lat, 3, axis=-1)
    sh_c, sc_c, gt_c = jnp.split(mod_ctx, 3, axis=-1)
    xn = _rms_norm(x, g_norm) * (1 + sc[:, None]) + sh[:, None]
    cn = _rms_norm(ctx, g_norm) * (1 + sc_c) + sh_c
    (au, av, az, bq, bk, bv, bz, cq, ck, cv, co, cz, cg) = _split_proj(xn @ w_in)
    (au_c, av_c, az_c, bq_c, bk_c, bv_c, bz_c, cq_c, ck_c, cv_c, co_c, cz_c, cg_c) = _split_proj(cn @ w_in)

    def heads_q(t):
        return _rms_norm(t.reshape(t.shape[:2] + (B_KV_HEADS, B_GROUP, HEAD_DIM)), g_q)

    def heads_k(t):
        return _rms_norm(t.reshape(t.shape[:2] + (B_KV_HEADS, HEAD_DIM)), g_k)

    q_lat = _apply_rope(heads_q(bq), tabs)
    k_lat = _apply_rope(heads_k(bk), tabs)
    v_lat = bv.reshape(B_, S, B_KV_HEADS, HEAD_DIM)
    k_ctx = heads_k(bk_c)
    v_ctx = bv_c.reshape(B_, Lc, B_KV_HEADS, HEAD_DIM)
    k_all = jnp.concatenate([k_ctx, k_lat], axis=1)
    v_all = jnp.concatenate([v_ctx, v_lat], axis=1)
    q_blocks = q_lat.reshape(B_, S // Q_BLOCK, Q_BLOCK, B_KV_HEADS, B_GROUP, HEAD_DIM).swapaxes(0, 1)
    o_lat = lax.map(lambda qb: _attend(qb, k_all, v_all), q_blocks)
    y_b = o_lat.swapaxes(0, 1).reshape(B_, S, B_WIDTH) * jax.nn.silu(bz)

    def mheads(t):
        return t.reshape(t.shape[:2] + (C_HEADS, HEAD_DIM))

    def gates(g):
        g = g.astype(jnp.float32) + b_gates.astype(jnp.float32)
        i_f, f_f, i_b, f_b = jnp.split(g, 4, axis=-1)
        return i_f, jax.nn.log_sigmoid(f_f), i_b, jax.nn.log_sigmoid(f_b)

    def flip(t):
        return jnp.flip(t, axis=1)

    ql, kl, vl = mheads(cq), mheads(ck), mheads(cv)
    qc_, kc_, vc_ = mheads(cq_c), mheads(ck_c), mheads(cv_c)
    li_f, lf_f, li_b, lf_b = gates(cg)
    ci_f, cf_f, ci_b, cf_b = gates(cg_c)
    st0 = (jnp.zeros((B_, C_HEADS, HEAD_DIM, HEAD_DIM), jnp.float32),
           jnp.zeros((B_, C_HEADS, HEAD_DIM), jnp.float32),
           jnp.zeros((B_, C_HEADS), jnp.float32))
    h_cf, st_f = _mlstm_scan(qc_, kc_, vc_, ci_f, cf_f, st0, with_ctx_out)
    h_cb, st_b = _mlstm_scan(flip(qc_), flip(kc_), flip(vc_), flip(ci_b), flip(cf_b), st0, with_ctx_out)
    h_lf, _ = _mlstm_scan(ql, kl, vl, li_f, lf_f, st_f, True)
    h_lb, _ = _mlstm_scan(flip(ql), flip(kl), flip(vl), flip(li_b), flip(lf_b), st_b, True)
    g_hd = g_head.reshape(C_HEADS, HEAD_DIM)
    h_lat = _rms_norm(h_lf + flip(h_lb), g_hd).reshape(B_, S, C_WIDTH)
    y_c = jax.nn.sigmoid(co) * h_lat * jax.nn.silu(cz)

    y_a = _gmlp_branch(au, av, az, w_s, b_s)

    x = x + gt[:, None] * (jnp.concatenate([y_a, y_b, y_c], axis=-1) @ w_out)

    if with_ctx_out:
        o_ctx = _attend(heads_q(bq_c), k_ctx, v_ctx).reshape(B_, Lc, B_WIDTH)
        y_b_c = o_ctx * jax.nn.silu(bz_c)
        h_ctx = _rms_norm(h_cf + flip(h_cb), g_hd).reshape(B_, Lc, C_WIDTH)
        y_c_c = jax.nn.sigmoid(co_c) * h_ctx * jax.nn.silu(cz_c)
        y_a_c = _gmlp_branch(au_c, av_c, az_c, w_s, b_s)
        ctx = ctx + gt_c * (jnp.concatenate([y_a_c, y_b_c, y_c_c], axis=-1) @ w_out)
    return x, ctx


def setup_inputs(seed: int = 0) -> dict:
    key = jax.random.key(seed)
    ks = jax.random.split(key, 16)
    f32 = jnp.float32

    def nrm(k, shape, scale):
        return jax.random.normal(k, shape, f32) * scale

    f_bias = jnp.linspace(F_BIAS_LO, F_BIAS_HI, C_HEADS, dtype=f32)
    zeros_h = jnp.zeros((C_HEADS,), f32)
    gate_base = jnp.concatenate([zeros_h, f_bias, zeros_h, f_bias])
    return {
        'x': nrm(ks[0], (BATCH, SEQ, D_MODEL), 1.0),
        'c': nrm(ks[1], (BATCH, D_MODEL), 1.0),
        'ctx': nrm(ks[2], (BATCH, CTX_LEN, D_MODEL), 1.0),
        'c_ctx': nrm(ks[3], (D_MODEL,), 1.0),
        'w_ada': nrm(ks[4], (DEPTH, D_MODEL, 3 * D_MODEL), 0.5 * D_MODEL ** -0.5),
        'b_ada': nrm(ks[5], (DEPTH, 3 * D_MODEL), 0.01),
        'g_norm': 1.0 + nrm(ks[6], (DEPTH, D_MODEL), 0.02),
        'w_in': nrm(ks[7], (DEPTH, D_MODEL, D_IN), D_MODEL ** -0.5),
        'w_s': nrm(ks[8], (DEPTH, A_GROUPS, CHUNK, CHUNK), CHUNK ** -0.5),
        'b_s': 1.0 + nrm(ks[9], (DEPTH, A_GROUPS, CHUNK), 0.02),
        'g_q': 1.0 + nrm(ks[10], (DEPTH, HEAD_DIM), 0.02),
        'g_k': 1.0 + nrm(ks[11], (DEPTH, HEAD_DIM), 0.02),
        'b_gates': gate_base + nrm(ks[12], (DEPTH, 4 * C_HEADS), 0.1),
        'g_head': 1.0 + nrm(ks[13], (DEPTH, C_WIDTH), 0.02),
        'w_out': nrm(ks[14], (DEPTH, D_MIX, D_MODEL), D_MIX ** -0.5),
        'g_final': 1.0 + nrm(ks[15], (D_MODEL,), 0.02),
    }


def reference(x, c, ctx, c_ctx, w_ada, b_ada, g_norm, w_in, w_s, b_s, g_q, g_k, b_gates, g_head, w_out,
              g_final):
    ROWS = x.shape[1] // GRID_W
    tabs = _rope_tables(ROWS)
    silu_c = jax.nn.silu(c)
    silu_cc = jax.nn.silu(c_ctx)
    for l in range(DEPTH):
        mod_lat = silu_c @ w_ada[l] + b_ada[l]
        mod_ctx = silu_cc @ w_ada[l] + b_ada[l]
        x, ctx = _layer(x, ctx, mod_lat, mod_ctx, tabs, g_norm[l], w_in[l], w_s[l], b_s[l], g_q[l], g_k[l],
                        b_gates[l], g_head[l], w_out[l], with_ctx_out=(l < DEPTH - 1))
    return _rms_norm(x, g_final)
```

```python
import numpy as np
import concourse.bass as bass
import concourse.mybir as mybir
from concourse.bass_utils import run_bass_kernel_spmd

F32 = mybir.dt.float32
BF16 = mybir.dt.bfloat16
AF = mybir.ActivationFunctionType
ALU = mybir.AluOpType
AX = mybir.AxisListType

D = 1024
KC = 8
DIN = 3344
EPS = 1e-6
ENGS = ("pe", "act", "dve", "pool", "sp")
SEM_CHUNK = 8000


class _Op:
    __slots__ = ("eng", "fn", "deps", "marked", "rank", "dma", "dgroup")

    def __init__(self, eng, fn):
        self.eng = eng
        self.fn = fn
        self.deps = []
        self.marked = False
        self.rank = None
        self.dma = None
        self.dgroup = None


class _DmaSem:
    def __init__(self, name):
        self.name = name
        self.count = 0
        self.last = None


class Sched:
    def __init__(self, nc):
        self.nc = nc
        self.ops = {e: [] for e in ENGS}
        self.last_writer = {}
        self.readers = {}
        self.dma_sems = {}

    def _dep(self, op, prod):
        if prod is None or prod is op:
            return
        if prod.eng == op.eng and prod.dma is None and op.dma is None and op.eng == "pe":
            return
        snap = 0
        if prod.dma is not None:
            snap = prod.dma.count - (1 if op.dma is prod.dma else 0)
        op.deps.append((prod, snap))

    def op(self, eng, fn, reads=(), writes=(), dma=None):
        o = _Op(eng, fn)
        if dma is not None:
            ds = self.dma_sems.get(dma)
            if ds is None:
                ds = self.dma_sems[dma] = _DmaSem(dma)
            ds.count += 1
            ds.last = o
            o.dma = ds
            o.dgroup = ds.count
        self.ops[eng].append(o)
        psr = [k for k in reads if isinstance(k, tuple) and k[0] == "ps"]
        if psr:
            reads = [k for k in reads if k not in psr]
            writes = list(writes) + [k for k in psr if k not in writes]
        for k in reads:
            w = self.last_writer.get(k)
            if w is not None:
                self._dep(o, w)
            self.readers.setdefault(k, []).append(o)
        for k in writes:
            w = self.last_writer.get(k)
            if w is not None:
                self._dep(o, w)
            for r in self.readers.get(k, ()):
                if r is o:
                    continue
                self._dep(o, r)
            self.last_writer[k] = o
            self.readers[k] = []
        return o

    def barrier(self):
        lasts = []
        for e in ENGS:
            for o_ in reversed(self.ops[e]):
                if not getattr(o_.fn, "_is_nop", False):
                    lasts.append(o_)
                    break
        dl = [ds.last for ds in self.dma_sems.values() if ds.last is not None]
        for e in ENGS:
            fn = lambda h: h.nop()
            fn._is_nop = True
            o = _Op(e, fn)
            self.ops[e].append(o)
            for p in lasts + dl:
                if p.eng == e and p.dma is None:
                    continue
                o.deps.append((p, p.dma.count if p.dma is not None else 0))

    def emit(self):
        nc = self.nc
        for e in ENGS:
            for o in self.ops[e]:
                for p, _ in o.deps:
                    p.marked = True
        eng_sems = {}
        for e in ENGS:
            r = 0
            for o in self.ops[e]:
                if o.dma is None and o.marked:
                    o.rank = r
                    r += 1
            eng_sems[e] = [nc.alloc_semaphore(f"prog_{e}_{i}") for i in range(max((r + SEM_CHUNK - 1) // SEM_CHUNK, 1))]
        dsem = {name: nc.alloc_semaphore(f"dma_{name}") for name in self.dma_sems}

        def make(e):
            def run(h):
                waited = {}
                for o in self.ops[e]:
                    need = {}
                    for p, snap in o.deps:
                        if p.dma is not None:
                            base = ("d", p.dma.name)
                            tot = max(p.dgroup, snap) * 16
                            sem, val = dsem[p.dma.name], tot
                        else:
                            base = ("e", p.eng)
                            tot = p.rank + 1
                            sem, val = eng_sems[p.eng][p.rank // SEM_CHUNK], (p.rank % SEM_CHUNK) + 1
                        if waited.get(base, 0) >= tot:
                            continue
                        if need.get(base, (0,))[0] < tot:
                            need[base] = (tot, sem, val)
                    for base, (tot, sem, val) in need.items():
                        h.wait_ge(sem, val)
                        waited[base] = tot
                    ins = o.fn(h)
                    if o.dma is not None:
                        ins.then_inc(dsem[o.dma.name], 16)
                    elif o.marked:
                        ins.then_inc(eng_sems[e][o.rank // SEM_CHUNK], 1)
            return run

        return {e: make(e) for e in ENGS}


class Alloc:
    def __init__(self, nc, base=16512, limit=229376 - 64):
        self.nc = nc
        self.off = base
        self.limit = limit
        self.n = 0

    def __call__(self, name, shape, dtype):
        sz = int(np.prod(shape[1:])) * (4 if dtype == F32 else 2)
        off = (self.off + 31) // 32 * 32
        assert off + sz <= self.limit, f"SBUF overflow at {name}: {off}+{sz} > {self.limit}"
        self.off = off + sz
        self.peak = max(getattr(self, "peak", 0), self.off)
        self.n += 1
        return self.nc.alloc_sbuf_tensor_at(f"{name}_{self.n}", list(shape), dtype, offset=off)

    def mark(self):
        return self.off

    def release(self, m):
        self.off = m


class _Stop(Exception):
    pass


class Cfg:
    def __init__(self, S=2048, CTX=256, DEPTH=4, NB=2, debug=(), stop=None):
        self.S, self.CTX, self.DEPTH, self.NB = S, CTX, DEPTH, NB
        self.T = S + CTX
        self.NT = self.T // 128
        self.NCX = CTX // 128
        self.NR = NB + 1
        self.debug = tuple(debug)
        self.stop = stop
        blocks = []
        for (st, ln) in ((0, CTX), (CTX, S)):
            o = 0
            while o < ln:
                n = min(512, ln - o)
                blocks.append((st + o, n))
                o += n
        self.blocks = blocks


def _consts(cfg):
    S = cfg.S
    c = {}
    c["ident"] = np.eye(128, dtype=np.float32)
    tri = (np.arange(128)[:, None] <= np.arange(128)[None, :]).astype(np.float32)
    c["trif"] = tri
    c["trib"] = np.ascontiguousarray(tri.T)
    bo = np.zeros((128, 128), np.float32)
    bo[:64, :64] = 1
    bo[64:, 64:] = 1
    c["blockones"] = bo
    rot = np.zeros((128, 128), np.float32)
    for i in range(128):
        if (i % 32) < 16:
            rot[i + 16, i] = -1.0
        else:
            rot[i - 16, i] = 1.0
    c["rot"] = rot
    freqs = (10000.0 ** (-np.arange(16, dtype=np.float32) / 16)).astype(np.float32)
    s = np.arange(S)
    row = (s // 64).astype(np.float32)
    col = (s % 64).astype(np.float32)
    ang = np.zeros((128, S), np.float32)
    for p in range(128):
        d = p % 64
        if d < 32:
            ang[p] = row * freqs[d % 16]
        else:
            ang[p] = col * freqs[(d - 32) % 16]
    c["cosT"] = np.cos(ang).astype(np.float32)
    c["sinT"] = np.sin(ang).astype(np.float32)
    return c


def _vecT(v, nch):
    return np.ascontiguousarray(v.reshape(nch, 128).T)


def build(cfg):
    S_, CTX, DEPTH, NB, T, NT, NCX, NR = cfg.S, cfg.CTX, cfg.DEPTH, cfg.NB, cfg.T, cfg.NT, cfg.NCX, cfg.NR
    nc = bass.Bass("TRN2", target_bir_lowering=False)
    SC = Sched(nc)
    A = Alloc(nc)

    def din(name, shape):
        return nc.dram_tensor(name, list(shape), F32, kind="ExternalInput").ap()

    x_d = din("x", [NB, S_, D])
    ctx_d = din("ctx", [NB, CTX, D])
    cT_d = din("cT", [128, KC * NR])
    wada_d = din("w_ada", [DEPTH, D, 3 * D])
    bada_d = din("b_adaT", [DEPTH, 128, 24])
    gnorm_d = din("g_normT", [DEPTH, 128, KC])
    win_d = din("w_in", [DEPTH, D, DIN])
    ws_d = din("w_s", [DEPTH, 4, 128, 128])
    bsT_d = din("bsT", [DEPTH, 128, 256])
    gq_d = din("gq", [DEPTH, 128, 1])
    gk_d = din("gk", [DEPTH, 128, 1])
    bg_d = din("bgate", [DEPTH, 128, 16])
    gh_d = din("ghead", [DEPTH, 128, 256])
    wout_d = din("w_out", [DEPTH, D, D])
    gfin_d = din("g_finalT", [128, KC])
    ident_d = din("ident", [128, 128])
    trif_d = din("trif", [128, 128])
    trib_d = din("trib", [128, 128])
    bones_d = din("blockones", [128, 128])
    rot_d = din("rot", [128, 128])
    cos_d = din("cosT", [128, S_])
    sin_d = din("sinT", [128, S_])
    out_d = nc.dram_tensor("out", [NB, S_, D], F32, kind="ExternalOutput").ap()
    xT_d = nc.dram_tensor("xT_scratch", [NB, D, T], F32, kind="Internal").ap()
    dbg_d = {}

    PSALL = nc.alloc_psum_tensor("psall", [128, 7 * 512], F32)
    PS = [PSALL[:, i * 512:(i + 1) * 512] for i in range(7)]
    PSB = nc.alloc_psum_tensor("psbf", [128, 1024], BF16)

    def pk(i):
        return ("ps", i)

    ident = A("ident", [128, 128], F32)
    trif = A("trif", [128, 128], F32)
    trib = A("trib", [128, 128], F32)
    maskf = A("maskf", [128, 4, 128], BF16)
    maskb = A("maskb", [128, 4, 128], BF16)
    identb = A("identb", [128, 128], BF16)
    bones = A("bones", [128, 128], BF16)
    rotb = A("rotb", [128, 128], BF16)
    ones_b = A("ones_b", [128, 128], BF16)
    ones_f = A("ones_f", [128, 128], F32)
    cst = A("cst", [128, 4], F32)
    zero4 = A("zero4", [128, 4], F32)
    modt = A("modt", [128, DEPTH, 24, NR], F32)
    gmod = A("gmod", [128, DEPTH, KC, NR], F32)
    gnorm = A("gnorm", [128, DEPTH, KC], F32)
    bada = A("bada", [128, DEPTH, 24], F32)
    gfin = A("gfin", [128, KC], F32)
    gqk = A("gqk", [128, DEPTH, 2], F32)
    yT = A("yT", [128, 8, T], BF16)
    xb0 = A("xb0", [128, KC, 512], F32)
    xn = A("xn", [128, KC, 512], BF16)
    xn2_own = A("xn2", [128, KC, 512], BF16) if 2 * T < KC * 512 else None
    tmpk = [A(f"tmpk{i}", [128, 512], F32) for i in range(2)]
    sqk = [A(f"sqk{i}", [128, 512], BF16) for i in range(2)]
    rs_t = A("rs_t", [128, 512], F32)
    rstd = A("rstd", [128, 512], F32)
    WR_mark0 = A.mark()
    Wreg = A("Wreg", [128, KC, 2048], BF16)
    BC0 = A.mark()

    def eps_ap(n=128):
        return cst[0:n, 0:1]

    def one_ap(n=128):
        return cst[0:n, 1:2]

    def dma(q, out, in_, sem, reads=(), writes=()):
        SC.op(q, lambda h: h.dma_start(out=out, in_=in_), reads=reads, writes=writes, dma=sem)

    def act(out, in_, func, reads, writes, bias=None, scale=None):
        kw = {}
        if bias is not None:
            kw["bias"] = bias
        if scale is not None:
            kw["scale"] = scale
        SC.op("act", lambda h: h.activation(out=out, in_=in_, func=func, **kw), reads=reads, writes=writes)

    def tt(out, in0, in1, op, reads, writes, eng="dve"):
        SC.op(eng, lambda h: h.tensor_tensor(out=out, in0=in0, in1=in1, op=op), reads=reads, writes=writes)

    def ts(out, in0, s1, s2, op0, op1, reads, writes):
        if op1 is None:
            SC.op("dve", lambda h: h.tensor_scalar(out=out, in0=in0, scalar1=s1, scalar2=None, op0=op0), reads=reads, writes=writes)
        else:
            SC.op("dve", lambda h: h.tensor_scalar(out=out, in0=in0, scalar1=s1, scalar2=s2, op0=op0, op1=op1), reads=reads, writes=writes)

    def stt(out, in0, scalar, in1, op0, op1, reads, writes):
        SC.op("dve", lambda h: h.scalar_tensor_tensor(out=out, in0=in0, scalar=scalar, in1=in1, op0=op0, op1=op1),
              reads=reads, writes=writes)

    def vcopy(out, in_, reads, writes, eng="dve"):
        SC.op(eng, lambda h: h.tensor_copy(out=out, in_=in_), reads=reads, writes=writes)

    def recip(out, in_, reads, writes):
        SC.op("dve", lambda h: h.reciprocal(out=out, in_=in_), reads=reads, writes=writes)

    def memset(ap, val, writes):
        SC.op("dve", lambda h: h.memset(ap, val), writes=writes)

    def mm(out, lhsT, rhs, start, stop, reads, writes):
        SC.op("pe", lambda h: h.matmul(out, lhsT=lhsT, rhs=rhs, start=start, stop=stop), reads=reads, writes=writes)

    def tr(out, in_, idn, reads, writes):
        SC.op("pe", lambda h: h.transpose(out, in_, idn), reads=reads, writes=writes)

    def reduce(out, in_, op, reads, writes):
        SC.op("dve", lambda h: h.tensor_reduce(out=out, in_=in_, axis=AX.X, op=op), reads=reads, writes=writes)

    def chk(name):
        if cfg.stop == name:
            raise _Stop()

    def dbg(name, src_ap, shape, reads):
        if name in cfg.debug:
            if name not in dbg_d:
                dbg_d[name] = nc.dram_tensor("dbg_" + name, list(shape), src_ap.dtype, kind="ExternalOutput").ap()
            dma("sp", dbg_d[name], src_ap, "dbg", reads=reads, writes=[("dbg", name)])

    try:
        dma("sp", ident[:], ident_d, "c0", writes=["ident"])
        dma("sp", trif[:], trif_d, "c0", writes=["trif"])
        dma("sp", trib[:], trib_d, "c0", writes=["trib"])
        for hh in range(4):
            dma("pool", maskf[:, hh, :], trif_d, "c1", writes=["maskf"])
            dma("pool", maskb[:, hh, :], trib_d, "c1", writes=["maskb"])
        dma("pool", identb[:], ident_d, "c1", writes=["identb"])
        dma("pool", bones[:], bones_d, "c1", writes=["bones"])
        dma("pool", rotb[:], rot_d, "c1", writes=["rotb"])
        dma("sp", gnorm[:], gnorm_d.rearrange("l p k -> p l k"), "c0", writes=["gnorm"])
        dma("sp", bada[:], bada_d.rearrange("l p k -> p l k"), "c0", writes=["bada"])
        dma("sp", gfin[:], gfin_d, "c0", writes=["gfin"])
        for l in range(DEPTH):
            dma("sp", gqk[:, l, 0:1], gq_d[l], "c0", writes=["gqk"])
            dma("sp", gqk[:, l, 1:2], gk_d[l], "c0", writes=["gqk"])
        memset(ones_b[:], 1.0, ["ones_b"])
        memset(ones_f[:], 1.0, ["ones_f"])
        memset(cst[:, 0:1], EPS, ["cst"])
        memset(cst[:, 1:2], 1.0, ["cst"])
        memset(cst[:, 2:4], 0.0, ["cst"])
        memset(zero4[:], 0.0, ["zero4"])

        chk('consts')
        m0 = A.mark()
        cTs = A("cTs", [128, KC, NR], F32)
        csb = A("csb", [128, KC, NR], BF16)
        dma("sp", cTs[:].rearrange("p k r -> p (k r)"), cT_d, "c0", writes=["cTs"])
        act(csb[:], cTs[:], AF.Silu, ["cTs"], ["csb"])
        wada = Wreg

        def mod_dma(l, base, wkeys, m3):
            dma("pool", wada[:, :, base:base + 1024], wada_d[l, :, m3 * 1024:(m3 + 1) * 1024].rearrange("(k p) n -> p k n", p=128),
                "wada", writes=wkeys)

        def mod_mm(l, base, wkeys, bank, m3):
            for j in range(8):
                ch = m3 * 8 + j
                for k in range(KC):
                    mm(PS[bank][:, ch * NR:(ch + 1) * NR], wada[:, k, base + j * 128:base + (j + 1) * 128], csb[:, k, :],
                       k == 0, k == KC - 1, wkeys + ["csb"], [pk(bank)])

        def mod_layer(l, base, wkeys, bank):
            for m3 in range(3):
                mod_dma(l, base, wkeys, m3)
                mod_mm(l, base, wkeys, bank, m3)
            mod_fin(l, bank)

        def mod_fin(l, bank):
            tt(modt[:, l, :, :], PS[bank][:, 0:24 * NR].rearrange("p (c r) -> p c r", r=NR),
               bada[:, l, :].unsqueeze(2).to_broadcast([128, 24, NR]), ALU.add, [pk(bank), "bada"], [("modt", l)])
            ts(gmod[:, l, :, :], modt[:, l, 8:16, :], 1.0, None, ALU.add, None, [("modt", l)], [("gmod", l)])
            tt(gmod[:, l, :, :], gmod[:, l, :, :], gnorm[:, l, :].unsqueeze(2).to_broadcast([128, KC, NR]), ALU.mult,
               [("gmod", l), "gnorm"], [("gmod", l)])


        chk('mod')
        m0 = A.mark()
        intile = [A(f"intile{i}", [128, D], F32) for i in range(2)]
        mod_dma(0, 0, ["Wlo_a", "Wlo_q"], 0)
        cnt = 0
        for b in range(NB):
            for bi, (t0, n) in enumerate(cfg.blocks):
                for ti in range(n // 128):
                    tok = t0 + ti * 128
                    src = ctx_d[b, tok:tok + 128, :] if tok < CTX else x_d[b, tok - CTX:tok - CTX + 128, :]
                    it = intile[cnt % 2]
                    ik = ("intile", cnt % 2)
                    dma("pool", it[:], src, f"in{cnt % 2}", writes=[ik])
                    for k in range(KC):
                        bank = 1 + (k // 4)
                        tr(PS[bank][:, (k % 4) * 128:(k % 4 + 1) * 128], it[:, k * 128:(k + 1) * 128], ident[:],
                           [ik, "ident"], [pk(bank)])
                    vcopy(xb0[:, 0:4, ti * 128:(ti + 1) * 128], PS[1][:].rearrange("p (k t) -> p k t", k=4), [pk(1)], ["xb0"])
                    SC.op("act", lambda h, o=xb0[:, 4:8, ti * 128:(ti + 1) * 128], i=PS[2][:].rearrange("p (k t) -> p k t", k=4):
                          h.copy(out=o, in_=i), reads=[pk(2)], writes=["xb0"])
                    cnt += 1
                dma("sp", xT_d[b, :, t0:t0 + n].rearrange("(k p) t -> p k t", p=128), xb0[:, :, 0:n], "xst",
                    reads=["xb0"], writes=[("xd", b, bi)])
        A.release(m0)
        mod_mm(0, 0, ["Wlo_a", "Wlo_q"], 0, 0)
        for m3_ in (1, 2):
            mod_dma(0, 0, ["Wlo_a", "Wlo_q"], m3_)
            mod_mm(0, 0, ["Wlo_a", "Wlo_q"], 0, m3_)
        mod_fin(0, 0)
        SC.barrier()

        chk('setup')
        xstate = {"loaded": None}

        def xload(b, bi):
            if xstate["loaded"] == (b, bi):
                return
            t0_, n_ = cfg.blocks[bi]
            dma("sp", xb0[:, :, 0:n_], xT_d[b, :, t0_:t0_ + n_].rearrange("(k p) t -> p k t", p=128), "xld",
                reads=[("xd", b, bi)], writes=["xb0"])
            xstate["loaded"] = (b, bi)

        if 2 * T >= KC * 512:
            xn2 = yT[:, 6:8, :].rearrange("p a t -> p (a t)")[:, 0:KC * 512].rearrange("p (k n) -> p k n", k=KC)
        else:
            xn2 = xn2_own[:]
        xnbuf = [xn[:], xn2]

        early = {"p1": None}

        def norm_block(b, l, bi, final=False, nxt=None, slot=0, preloaded=False):
            t0, n = cfg.blocks[bi]
            r = NB if t0 < CTX else b
            if not preloaded:
                xload(b, bi)
            for k in range(KC):
                sk = ("sqk", k % 2)
                act(sqk[k % 2][:, 0:n], xb0[:, k, 0:n], AF.Square, ["xb0"], [sk])
                mm(PS[6][:, 0:n], ones_b[:], sqk[k % 2][:, 0:n], k == 0, k == KC - 1, [sk, "ones_b"], [pk(6)])
            act(rs_t[:, 0:n], PS[6][:, 0:n], AF.Ln, [pk(6), "cst"], ["rs_t"], bias=eps_ap(), scale=1.0 / D)
            act(rstd[:, 0:n], rs_t[:, 0:n], AF.Exp, ["rs_t"], ["rstd"], scale=-0.5)
            if not final:
                for k in range(KC):
                    tk = ("tmpk", k % 2)
                    XE = "dve" if bi == 0 else "pool"
                    tt(tmpk[k % 2][:, 0:n], xb0[:, k, 0:n], rstd[:, 0:n], ALU.mult, ["xb0", "rstd"], [tk], eng=XE)
                    SC.op(XE, lambda h, o=xnbuf[slot][:, k, 0:n], i_=tmpk[k % 2][:, 0:n], s1=gmod[:, l, k, r:r + 1], s2=modt[:, l, k, r:r + 1]:
                          h.tensor_scalar(out=o, in0=i_, scalar1=s1, scalar2=s2, op0=ALU.mult, op1=ALU.add),
                          reads=[tk, ("gmod", l), ("modt", l)], writes=[("xn", slot)])
                if nxt is not None:
                    xload(b, nxt)
            return r, t0, n

        def wkey(col):
            return "Wlo_a" if col < 512 else ("Wlo_q" if col < 1024 else "Whi")

        def load_w_cols(dst_col, src_c0, ncols, l, sem):
            keys = sorted(set([wkey(dst_col), wkey(dst_col + ncols - 1)]))
            dma("pool", Wreg[:, :, dst_col:dst_col + ncols],
                win_d[l, :, src_c0:src_c0 + ncols].rearrange("(k p) n -> p k n", p=128), "W_" + keys[0], writes=keys)

        def load_w1(l, part):
            if part == "lo":
                load_w_cols(0, 0, 256, l, "W")
                load_w_cols(256, 512, 256, l, "W")
                for j in range(4):
                    load_w_cols(512 + 128 * j, 768 + 64 * j, 64, l, "W")
                    load_w_cols(512 + 128 * j + 64, 768 + 64 * (4 + j), 64, l, "W")
            else:
                load_w_cols(1024, 1280, 128, l, "W")
                for j in range(4):
                    load_w_cols(1152 + 128 * j, 1536 + 64 * j, 64, l, "W")
                    load_w_cols(1152 + 128 * j + 64, 1536 + 64 * (4 + j), 64, l, "W")
                load_w_cols(1664, 256, 256, l, "W")
                load_w_cols(1920, 1408, 128, l, "W")

        units = [(b_, l_) for b_ in range(NB) for l_ in range(DEPTH)]
        w1hi_ready = {"u": None}

        for b in range(NB):
            for l in range(DEPTH):
                um = A.mark()
                load_w1(l, "lo")
                if w1hi_ready["u"] != (b, l):
                    load_w1(l, "hi")

                qT = A("qT", [128, 4, T], BF16)
                kT = A("kT", [128, 2, T], BF16)
                vB = A("vB", [128, NT, 2, 128], BF16)
                cosT = A("cosT", [128, S_], F32)
                sinT = A("sinT", [128, S_], F32)
                wsraw = A("wsraw", [128, 4, 128], F32)
                wsT = A("wsT", [128, 4, 128], BF16)
                bsT = A("bsT", [128, 256], F32)
                gu = A("gu", [128, 2, 512], BF16)
                vg = A("vg", [128, 4, 256], F32)
                vsq4 = A("vsq4", [128, 4, 256], F32)
                vn = A("vn", [128, 16, 64], BF16)
                st4 = A("st4", [128, 64], F32)
                sqq = A("sqq", [128, 512], BF16)
                rq = A("rq", [128, 512], F32)
                qnb = A("qnb", [128, 512], BF16)
                t1 = A("t1", [128, 512], F32)
                t2 = A("t2", [128, 512], F32)
                gtmp = A("gtmp", [128, 256], F32)
                pTall = A("pTall", [128, 6, 512], BF16)
                pT = [pTall[:, i, :] for i in range(6)]
                vsq_flat = vsq4[:].rearrange("p a b -> p (a b)")
                rden = [vsq_flat[:, 0:512], vsq_flat[:, 512:1024]]
                tmpo = A("tmpo", [128, 512], F32)

                dma("sp", wsraw[:], ws_d[l].rearrange("g p q -> p g q"), "c0", writes=["wsraw"])
                dma("sp", bsT[:], bsT_d[l], "c0", writes=["bsT"])
                dma("sp", cosT[:], cos_d, "cs", writes=["cosT"])
                dma("sp", sinT[:], sin_d, "cs", writes=["sinT"])
                memset(vB[:], 1.0, [("vB", t) for t in range(NT)])
                memset(kT[:], 0.0, [("kT", t) for t in range(NT)])
                for g in range(4):
                    tr(PS[5][:, g * 128:(g + 1) * 128], wsraw[:, g, :], ident[:], ["wsraw", "ident"], [pk(5)])
                vcopy(wsT[:], PS[5][:].rearrange("p (g q) -> p g q", g=4), [pk(5)], ["wsT"])

                nblk = len(cfg.blocks)
                fcnt = [0]
                if early["p1"] == (b, l):
                    xload(b, 1 % nblk)
                else:
                    norm_block(b, l, 0, nxt=1 % nblk, slot=0)
                for bi in range(nblk):
                    t0, n = cfg.blocks[bi]
                    r = NB if t0 < CTX else b
                    xnb = xnbuf[bi % 2]
                    xnk = ("xn", bi % 2)
                    lat = t0 >= CTX
                    s0 = t0 - CTX
                    tiles = [t0 // 128 + i for i in range(n // 128)]
                    nt_ = len(tiles)
                    for ti, tg in enumerate(tiles):
                        c0 = ti * 128
                        tb = 2 + (ti % 2)
                        for k in range(KC):
                            mm(PS[tb][:, 0:384], xnb[:, k, c0:c0 + 128], Wreg[:, k, 1664:2048], k == 0, k == KC - 1,
                               ["Whi", xnk], [pk(tb)])
                        act(vg[:, ti, :], PS[tb][:, 0:256], AF.Gelu_apprx_tanh, [pk(tb)], ["vg"])
                        SC.op("act", lambda h, o=vB[:, tg, 0, 0:64], i=PS[tb][:, 256:320]: h.copy(out=o, in_=i),
                              reads=[pk(tb)], writes=[("vB", tg)])
                        SC.op("act", lambda h, o=vB[:, tg, 1, 64:128], i=PS[tb][:, 320:384]: h.copy(out=o, in_=i),
                              reads=[pk(tb)], writes=[("vB", tg)])
                    fbanks = [0, 1, 5]
                    for ci in [0, 1, 2, 3, 9, 10, 11, 12]:
                        bank = fbanks[fcnt[0] % 3]
                        fcnt[0] += 1
                        for k in range(KC):
                            mm(PS[bank][:, 0:n], Wreg[:, k, ci * 128:(ci + 1) * 128], xnb[:, k, 0:n], k == 0, k == KC - 1,
                               [wkey(ci * 128), xnk], [pk(bank)])
                        src = PS[bank][:, 0:n]
                        if ci < 2:
                            act(gu[:, ci, 0:n], src, AF.Gelu_apprx_tanh, [pk(bank)], ["gu"])
                        elif ci < 4:
                            act(yT[:, ci - 2, t0:t0 + n], src, AF.Silu, [pk(bank)], [("yTa", t) for t in tiles])
                        elif ci >= 9:
                            act(yT[:, 2 + ci - 9, t0:t0 + n], src, AF.Silu, [pk(bank)],
                                [("yTb", t, kv) for t in tiles for kv in (0, 1)])
                        else:
                            isq = ci < 8
                            gcol = gqk[:, l, 0:1] if isq else gqk[:, l, 1:2]
                            dst = qT[:, ci - 4, t0:t0 + n] if isq else None
                            dkeys = [("qT", t) for t in tiles] if isq else [("kT", t) for t in tiles]
                            halves = [(slice(0, 128), dst)] if isq else [(slice(0, 64), kT[0:64, 0, t0:t0 + n]), (slice(64, 128), kT[64:128, 1, t0:t0 + n])]
                            act(sqq[:, 0:n], src, AF.Square, [pk(bank)], ["sqq"])
                            act(qnb[:, 0:n], src, AF.Identity, [pk(bank), "gqk"], ["qnb"], scale=gcol)
                            mm(PS[4][:, 0:n], bones[:], sqq[:, 0:n], True, True, ["sqq", "bones"], [pk(4)])
                            act(rs_t[:, 0:n], PS[4][:, 0:n], AF.Ln, [pk(4), "cst"], ["rs_t"], bias=eps_ap(), scale=1.0 / 64)
                            act(rq[:, 0:n], rs_t[:, 0:n], AF.Exp, ["rs_t"], ["rq"], scale=-0.5)
                            if not lat:
                                for hs_, d_ in halves:
                                    tt(d_, qnb[hs_, 0:n], rq[hs_, 0:n], ALU.mult, ["qnb", "rq"], dkeys)
                            else:
                                mm(PS[6][:, 0:n], rotb[:], qnb[:, 0:n], True, True, ["qnb", "rotb"], [pk(6)])
                                tt(t1[:, 0:n], qnb[:, 0:n], cosT[:, s0:s0 + n], ALU.mult, ["qnb", "cosT"], ["t1"])
                                tt(t2[:, 0:n], PS[6][:, 0:n], sinT[:, s0:s0 + n], ALU.mult, [pk(6), "sinT"], ["t2"])
                                tt(t1[:, 0:n], t1[:, 0:n], t2[:, 0:n], ALU.add, ["t1", "t2"], ["t1"])
                                for hs_, d_ in halves:
                                    tt(d_, t1[hs_, 0:n], rq[hs_, 0:n], ALU.mult, ["t1", "rq"], dkeys)
                    if bi + 1 < nblk:
                        norm_block(b, l, bi + 1, nxt=(bi + 2) % nblk, slot=(bi + 1) % 2)
                    ng = nt_ * 4
                    vgf = vg[:, 0:nt_, :].rearrange("p a (g d) -> p (a g) d", d=64)
                    vsf = vsq4[:, 0:nt_, :].rearrange("p a (g d) -> p (a g) d", d=64)
                    reduce(st4[:, 0:ng], vgf, ALU.add, ["vg"], ["st4"])
                    tt(vsf, vgf, vgf, ALU.mult, ["vg"], ["vsq4"])
                    reduce(st4[:, 16:16 + ng], vsf, ALU.add, ["vsq4"], ["st4"])
                    ts(st4[:, 0:ng], st4[:, 0:ng], 1.0 / 64, None, ALU.mult, None, ["st4"], ["st4"])
                    tt(st4[:, 32:32 + ng], st4[:, 0:ng], st4[:, 0:ng], ALU.mult, ["st4"], ["st4"])
                    stt(st4[:, 16:16 + ng], st4[:, 16:16 + ng], 1.0 / 64, st4[:, 32:32 + ng], ALU.mult, ALU.subtract, ["st4"], ["st4"])
                    act(st4[:, 32:32 + ng], st4[:, 16:16 + ng], AF.Ln, ["st4", "cst"], ["st4"], bias=eps_ap(), scale=1.0)
                    act(st4[:, 48:48 + ng], st4[:, 32:32 + ng], AF.Exp, ["st4"], ["st4"], scale=-0.5)
                    tt(vsf, vgf, st4[:, 0:ng].unsqueeze(2).to_broadcast([128, ng, 64]), ALU.subtract, ["vg", "st4"], ["vsq4"])
                    tt(vn[:, 0:ng, :], vsf, st4[:, 48:48 + ng].unsqueeze(2).to_broadcast([128, ng, 64]), ALU.mult,
                       ["vsq4", "st4"], ["vn"])
                    fbanks = [0, 1, 5]
                    for ci in [4, 5, 6, 7, 8]:
                        bank = fbanks[fcnt[0] % 3]
                        fcnt[0] += 1
                        for k in range(KC):
                            mm(PS[bank][:, 0:n], Wreg[:, k, ci * 128:(ci + 1) * 128], xnb[:, k, 0:n], k == 0, k == KC - 1,
                               [wkey(ci * 128), xnk], [pk(bank)])
                        src = PS[bank][:, 0:n]
                        if ci < 2:
                            act(gu[:, ci, 0:n], src, AF.Gelu_apprx_tanh, [pk(bank)], ["gu"])
                        elif ci < 4:
                            act(yT[:, ci - 2, t0:t0 + n], src, AF.Silu, [pk(bank)], [("yTa", t) for t in tiles])
                        elif ci >= 9:
                            act(yT[:, 2 + ci - 9, t0:t0 + n], src, AF.Silu, [pk(bank)],
                                [("yTb", t, kv) for t in tiles for kv in (0, 1)])
                        else:
                            isq = ci < 8
                            gcol = gqk[:, l, 0:1] if isq else gqk[:, l, 1:2]
                            dst = qT[:, ci - 4, t0:t0 + n] if isq else None
                            dkeys = [("qT", t) for t in tiles] if isq else [("kT", t) for t in tiles]
                            halves = [(slice(0, 128), dst)] if isq else [(slice(0, 64), kT[0:64, 0, t0:t0 + n]), (slice(64, 128), kT[64:128, 1, t0:t0 + n])]
                            act(sqq[:, 0:n], src, AF.Square, [pk(bank)], ["sqq"])
                            act(qnb[:, 0:n], src, AF.Identity, [pk(bank), "gqk"], ["qnb"], scale=gcol)
                            mm(PS[4][:, 0:n], bones[:], sqq[:, 0:n], True, True, ["sqq", "bones"], [pk(4)])
                            act(rs_t[:, 0:n], PS[4][:, 0:n], AF.Ln, [pk(4), "cst"], ["rs_t"], bias=eps_ap(), scale=1.0 / 64)
                            act(rq[:, 0:n], rs_t[:, 0:n], AF.Exp, ["rs_t"], ["rq"], scale=-0.5)
                            if not lat:
                                for hs_, d_ in halves:
                                    tt(d_, qnb[hs_, 0:n], rq[hs_, 0:n], ALU.mult, ["qnb", "rq"], dkeys)
                            else:
                                mm(PS[6][:, 0:n], rotb[:], qnb[:, 0:n], True, True, ["qnb", "rotb"], [pk(6)])
                                tt(t1[:, 0:n], qnb[:, 0:n], cosT[:, s0:s0 + n], ALU.mult, ["qnb", "cosT"], ["t1"])
                                tt(t2[:, 0:n], PS[6][:, 0:n], sinT[:, s0:s0 + n], ALU.mult, [pk(6), "sinT"], ["t2"])
                                tt(t1[:, 0:n], t1[:, 0:n], t2[:, 0:n], ALU.add, ["t1", "t2"], ["t1"])
                                for hs_, d_ in halves:
                                    tt(d_, t1[hs_, 0:n], rq[hs_, 0:n], ALU.mult, ["t1", "rq"], dkeys)
                    for ti, tg in enumerate(tiles):
                        c0 = ti * 128
                        gb_ = 2 + (ti % 2)
                        for g in range(4):
                            hp = 64 * (g % 2)
                            mm(PS[gb_][hp:hp + 64, (g // 2) * 128:(g // 2 + 1) * 128], vn[:, ti * 4 + g, :], wsT[:, g, :], True, True,
                               ["vn", "wsT"], [pk(gb_)])
                        tt(gtmp[:], PS[gb_][:, 0:256], bsT[:], ALU.add, [pk(gb_), "bsT"], ["gtmp"])
                        g3 = gtmp[:].rearrange("p (j q) -> p j q", j=2)
                        tt(g3, g3, gu[:, :, c0:c0 + 128], ALU.mult, ["gtmp", "gu"], ["gtmp"])
                        tt(yT[:, 0:2, tg * 128:(tg + 1) * 128], g3, yT[:, 0:2, tg * 128:(tg + 1) * 128], ALU.mult,
                           ["gtmp", ("yTa", tg)], [("yTa", tg)])
                dbg("qT", qT[:], [128, 4, T], [("qT", t) for t in range(NT)])
                dbg("kT", kT[:], [128, 2, T], [("kT", t) for t in range(NT)])
                dbg("yT_p1", yT[:], [128, 8, T], [("yTa", t) for t in range(NT)] + [("yTb", t, kv) for t in range(NT) for kv in (0, 1)])

                chk('p1')
                SC.barrier()
                norm_block(b, l, 0, nxt=1 % nblk, slot=0)
                load_w_cols(0, 2048, 512, l, "W")
                load_w_cols(512, 2304, 512, l, "W")
                load_w_cols(1024, 2816, 512, l, "W")
                load_w_cols(1536, 3328, 16, l, "W")
                its = []
                g_id = 0
                lat_q = list(range(NCX, NT))
                ctx_q = list(range(NCX))
                qorder = []
                while lat_q or ctx_q:
                    qorder += lat_q[:2]
                    lat_q = lat_q[2:]
                    if ctx_q:
                        qorder.append(ctx_q.pop(0))
                for kv in (0, 1):
                    for qb in qorder:
                        srange = list(range(NCX)) if qb < NCX else list(range(NT))
                        for i, sc in enumerate(srange):
                            its.append((kv, qb, i, sc, len(srange), 4 + (g_id % 3), g_id))
                        g_id += 1
                pending = []

                def epi_a(kv, qb, ob, gid):
                    o0 = 64 * kv
                    d0 = 64 * (1 - kv)
                    rd = rden[gid % 2]
                    recip(rd[d0:d0 + 64, :], PS[ob][d0:d0 + 64, :], [pk(ob)], [("rd", gid % 2, d0)])
                    dma("sp", rd[o0:o0 + 64, :], rd[d0:d0 + 64, :], "rdsw", reads=[("rd", gid % 2, d0)], writes=[("rd", gid % 2, o0)])

                def epi_b(kv, qb, ob, gid):
                    o0 = 64 * kv
                    rd = rden[gid % 2]
                    ysl = yT[o0:o0 + 64, 2:6, qb * 128:(qb + 1) * 128]
                    to3 = tmpo[o0:o0 + 64, :].rearrange("p (g q) -> p g q", g=4)
                    tt(to3, PS[ob][o0:o0 + 64, :].rearrange("p (g q) -> p g q", g=4), ysl, ALU.mult,
                       [pk(ob), ("yTb", qb, kv)], ["tmpo"])
                    tt(ysl, to3, rd[o0:o0 + 64, :].rearrange("p (g q) -> p g q", g=4), ALU.mult,
                       ["tmpo", ("rd", gid % 2, o0)], [("yTb", qb, kv)])

                NPT = len(pT)
                nit = len(its)

                def qk_pair(js):
                    args = []
                    rk, wk_ = [], []
                    for j in js:
                        kv, qb, i, sc, ns, ob, gid = its[j]
                        args.append((PS[j % 4][:].rearrange("p (g q) -> p g q", g=4), kT[:, kv, sc * 128:(sc + 1) * 128],
                                     qT[:, :, qb * 128:(qb + 1) * 128]))
                        rk += [("kT", sc), ("qT", qb)]
                        wk_.append(pk(j % 4))

                    def fn(h, args=args):
                        ins = None
                        for o_, l_, r_ in args:
                            ins = h.matmul(o_, lhsT=l_, rhs=r_, start=True, stop=True)
                        return ins
                    SC.op("pe", fn, reads=rk, writes=wk_)
                    if len(js) == 2:
                        b0, p0 = js[0] % 4, js[0] % NPT
                        act(pTall[:, p0:p0 + 2, :].rearrange("p a n -> p (a n)"), PSALL[:, b0 * 512:(b0 + 2) * 512], AF.Exp,
                            [pk(b0), pk(b0 + 1)], [("pT", p0), ("pT", p0 + 1)], scale=0.125)
                    else:
                        for j in js:
                            act(pT[j % NPT], PS[j % 4], AF.Exp, [pk(j % 4)], [("pT", j % NPT)], scale=0.125)

                def pv_pair(js):
                    args = []
                    rk, wk_ = [], []
                    for jj in js:
                        kv, qb, i, sc, ns, ob, gid = its[jj]
                        if i == 0:
                            for p in [p for p in pending if p[1][2] == ob]:
                                pending.remove(p)
                                epi_b(*p[1])
                        args.append((PS[ob][:], vB[:, sc, kv, :], pT[jj % NPT], i == 0, i == ns - 1))
                        rk += [("vB", sc), ("pT", jj % NPT)]
                        if pk(ob) not in wk_:
                            wk_.append(pk(ob))

                    def fn(h, args=args):
                        ins = None
                        for o_, l_, r_, st_, sp_ in args:
                            ins = h.matmul(o_, lhsT=l_, rhs=r_, start=st_, stop=sp_)
                        return ins
                    SC.op("pe", fn, reads=rk, writes=wk_)
                    for jj in js:
                        kv, qb, i, sc, ns, ob, gid = its[jj]
                        if i == ns - 1:
                            for p in [p for p in pending if p[1][3] % 2 == gid % 2]:
                                pending.remove(p)
                                epi_b(*p[1])
                            epi_a(kv, qb, ob, gid)
                            pending.append([3, (kv, qb, ob, gid)])

                for j in range(0, nit + 4, 2):
                    js = [x for x in (j, j + 1) if x < nit]
                    if js:
                        qk_pair(js)
                    pj = [x for x in (j - 4, j - 3) if 0 <= x < nit]
                    if pj:
                        pv_pair(pj)
                    for p in pending:
                        p[0] -= 1
                    while pending and pending[0][0] <= 0:
                        epi_b(*pending.pop(0)[1])
                while pending:
                    epi_b(*pending.pop(0)[1])
                dbg("yT_b", yT[:], [128, 8, T], [("yTa", t) for t in range(NT)] + [("yTb", t, kv) for t in range(NT) for kv in (0, 1)])
                SC.barrier()
                A.release(um)

                chk('b')
                qc = A("qc", [128, 4, T], BF16)
                kc = A("kc", [128, 2, T], BF16)
                ktok = A("ktok", [128, NT, 256], BF16)
                vC = A("vC", [128, NT, 4, 80], BF16)
                og = A("og", [128, NT, 256], BF16)
                hsum = A("hsum", [128, NT, 256], F32)
                gts = A("gts", [128, NT, 16], F32)
                sg = A("sg", [128, 512], F32)
                ogt = A("ogt", [128, 256], F32)
                G8 = [NT * 4]
                shp = [128, 2, NT, 4]
                SP = A("SP", shp, F32)
                TOT = A("TOT", shp, F32)
                BOF = A("BOF", shp, F32)
                BP = A("BP", shp, F32)
                GG = A("GG", shp, F32)
                CMR = A("CMR", shp, F32)
                RR = A("RR", shp, F32)
                RP = A("RP", shp, F32)
                WK = A("WK", shp, F32)
                FL = A("FL", shp, F32)
                LAM = A("LAM", shp, F32)
                TMPG = A("TMPG", shp, F32)
                LSEL = A("LSEL", [128, 2, NT, 2], F32)
                cm = A("cm", [128, 2], F32)
                Dg = A("Dg", [128, 128], F32)
                Sm = [A(f"Sm{i}", [128, 4, 128], BF16) for i in range(2)]
                vt = [A(f"vt{i}", [128, 4, 80], BF16) for i in range(2)]
                Cst = [A(f"Cst{i}", [128, 2, 80], F32) for i in range(2)]
                Cbf = [A(f"Cbf{i}", [128, 2, 80], BF16) for i in range(2)]
                dmx = [A(f"dmx{i}", [128, 8], F32) for i in range(2)]
                tmpx = [A(f"tmpx{i}", [128, 4, 64], F32) for i in range(2)]
                hq = A("hq", [128, 2, 256], F32)
                hst = A("hst", [128, 16], F32)
                yct = A("yct", [128, 2, 256], BF16)
                bgt = A("bgt", [128, 16], F32)
                ghd = A("ghd", [128, 256], F32)
                xb1 = None

                dma("sp", bgt[:], bg_d[l], "c0", writes=["bgt"])
                dma("sp", ghd[:], gh_d[l], "c0", writes=["ghd"])
                memset(vC[:], 1.0, [("vC", t) for t in range(NT)])
                memset(qc[:], 0.0, [("qc", t) for t in range(NT)])
                chk('p2a')

                for bi in range(nblk):
                    t0, n = cfg.blocks[bi]
                    r = NB if t0 < CTX else b
                    xnb = xnbuf[bi % 2]
                    xnk = ("xn", bi % 2)
                    tiles = [t0 // 128 + i for i in range(n // 128)]
                    for ci in range(4):
                        bank = ci % 2
                        for k in range(KC):
                            mm(PS[bank][:, 0:n], Wreg[:, k, ci * 128:(ci + 1) * 128], xnb[:, k, 0:n], k == 0, k == KC - 1,
                               ["Wlo_a", xnk], [pk(bank)])
                        if ci < 2:
                            act(qc[0:64, 2 * ci, t0:t0 + n], PS[bank][0:64, 0:n], AF.Identity, [pk(bank)], [("qc", t) for t in tiles], scale=0.125)
                            act(qc[64:128, 2 * ci + 1, t0:t0 + n], PS[bank][64:128, 0:n], AF.Identity, [pk(bank)], [("qc", t) for t in tiles], scale=0.125)
                        else:
                            vcopy(kc[:, ci - 2, t0:t0 + n], PS[bank][:, 0:n], [pk(bank)], [("kc", t) for t in tiles])
                    chk('p2b')
                    if bi + 1 < nblk:
                        norm_block(b, l, bi + 1, nxt=(bi + 2) % nblk, slot=(bi + 1) % 2)
                    for ti, tg in enumerate(tiles):
                        c0 = ti * 128
                        for k in range(KC):
                            mm(PS[2][:, 0:512], xnb[:, k, c0:c0 + 128], Wreg[:, k, 512:1024], k == 0, k == KC - 1, ["Wlo_q", xnk], [pk(2)])
                        for k in range(KC):
                            mm(PS[3][:, 0:512], xnb[:, k, c0:c0 + 128], Wreg[:, k, 1024:1536], k == 0, k == KC - 1, ["Whi", xnk], [pk(3)])
                        for k in range(KC):
                            mm(PS[4][:, 0:16], xnb[:, k, c0:c0 + 128], Wreg[:, k, 1536:1552], k == 0, k == KC - 1, ["Whi", xnk], [pk(4)])
                        chk('p2c')
                        SC.op("act", lambda h, o=ktok[:, tg, :], i=PS[2][:, 0:256]: h.copy(out=o, in_=i), reads=[pk(2)], writes=[("kt", tg)])
                        chk('p2d')
                        vcopy(vC[:, tg, :, 0:64], PS[2][:, 256:512].rearrange("p (h d) -> p h d", h=4), [pk(2)], [("vC", tg)])
                        chk('p2e')
                        act(sg[:], PS[3][:], AF.Sigmoid, [pk(3)], ["sg"])
                        chk('p2f')
                        tt(ogt[:], sg[:, 0:256], sg[:, 256:512], ALU.mult, ["sg"], ["ogt"])
                        tt(og[:, tg, :], ogt[:], PS[3][:, 256:512], ALU.mult, ["ogt", pk(3)], [("og", tg)])
                        chk('p2g')
                        vcopy(gts[:, tg, :], PS[4][:, 0:16], [pk(4)], ["gts"])
                        chk('p2h')

                defer_mod = (b == 0 and l + 1 < DEPTH)
                if defer_mod:
                    mod_dma(l + 1, 1024, ["Whi"], 0)
                dma("pool", Wreg[:, 0:2, 0:1024], wout_d[l, 0:256, :].rearrange("(k p) n -> p k n", p=128), "W", writes=["Wlo_a", "Wlo_q"])
                for j in range(4):
                    dma("pool", Wreg[0:64, 2 + j, 0:1024], wout_d[l, 256 + 64 * j:256 + 64 * j + 64, :], "W", writes=["Wlo_a", "Wlo_q"])
                    dma("pool", Wreg[64:128, 2 + j, 0:1024], wout_d[l, 256 + 64 * (4 + j):256 + 64 * (4 + j) + 64, :], "W", writes=["Wlo_a", "Wlo_q"])
                dma("pool", Wreg[:, 6:8, 0:1024], wout_d[l, 768:1024, :].rearrange("(k p) n -> p k n", p=128), "W", writes=["Wlo_a", "Wlo_q"])
                chk('p2')
                ordf = list(range(NT))
                ordb = list(range(NCX - 1, -1, -1)) + list(range(NT - 1, NCX - 1, -1))
                orders = [ordf, ordb]
                tt(gts[:], gts[:], bgt[:].unsqueeze(1).to_broadcast([128, NT, 16]), ALU.add, ["gts", "bgt"], ["gts"])
                for dr in range(2):
                    act(SP[:, dr, :, :], gts[:, :, 4 + 8 * dr:8 + 8 * dr], AF.Exp, ["gts"], ["SP"], scale=-1.0)
                act(SP[:], SP[:], AF.Ln, ["SP", "cst"], ["SP"], bias=one_ap(), scale=1.0)
                n4 = NT * 4
                for dr in range(2):
                    mm(PS[0][:, dr * n4:(dr + 1) * n4], (trif if dr == 0 else trib)[:], SP[:, dr, :, :].rearrange("p c h -> p (c h)"),
                       True, True, ["SP", "trif", "trib"], [pk(0)])
                mm(PS[1][:, 0:2 * n4], ones_f[:], SP[:].rearrange("p d c h -> p (d c h)"), True, True, ["SP", "ones_f"], [pk(1)])
                vcopy(TOT[:].rearrange("p d c h -> p (d c h)"), PS[1][:, 0:2 * n4], [pk(1)], ["TOT"])
                memset(BOF[:], 0.0, ["BOF"])
                for dr in range(2):
                    od = orders[dr]
                    for i in range(1, NT):
                        tt(BOF[:, dr, od[i], :], BOF[:, dr, od[i - 1], :], TOT[:, dr, od[i - 1], :], ALU.add, ["BOF", "TOT"], ["BOF"])
                tt(BP[:].rearrange("p d c h -> p (d c h)"), PS[0][:, 0:2 * n4], BOF[:].rearrange("p d c h -> p (d c h)"), ALU.add,
                   [pk(0), "BOF"], ["BP"])
                for dr in range(2):
                    tt(GG[:, dr, :, :], BP[:, dr, :, :], gts[:, :, 8 * dr:8 * dr + 4], ALU.add, ["BP", "gts"], ["GG"])
                for dr in range(2):
                    tr(PS[5][0:n4, dr * 128:(dr + 1) * 128], GG[:, dr, :, :].rearrange("p c h -> p (c h)"), ident[:], ["GG", "ident"], [pk(5)])
                for dr in range(2):
                    SC.op("dve", lambda h, o=cm[0:n4, dr:dr + 1], i=PS[5][0:n4, dr * 128:(dr + 1) * 128]:
                          h.tensor_reduce(out=o, in_=i, axis=AX.X, op=ALU.max), reads=[pk(5)], writes=["cm"])
                for dr in range(2):
                    ts(Dg[0:n4, 0:n4], ident[0:n4, 0:n4], cm[0:n4, dr:dr + 1], None, ALU.mult, None, ["cm", "ident"], ["Dg"])
                    mm(PS[0][:, 256 + dr * n4:256 + (dr + 1) * n4], ones_f[0:n4, :], Dg[0:n4, 0:n4], True, True, ["Dg", "ones_f"], [pk(0)])
                vcopy(CMR[:].rearrange("p d c h -> p (d c h)"), PS[0][:, 256:256 + 2 * n4], [pk(0)], ["CMR"])
                for dr in range(2):
                    od = orders[dr]
                    prev = zero4[:, 0:4]
                    for c in od:
                        vcopy(RP[:, dr, c, :], prev, ["RR", "zero4"], ["RP"])
                        tt(RR[:, dr, c, :], prev, CMR[:, dr, c, :], ALU.max, ["RR", "CMR", "zero4"], ["RR"])
                        prev = RR[:, dr, c, :]
                tt(TMPG[:], GG[:], RR[:], ALU.subtract, ["GG", "RR"], ["TMPG"])
                act(WK[:], TMPG[:], AF.Exp, ["TMPG"], ["WK"])
                tt(TMPG[:], BP[:], RR[:], ALU.subtract, ["BP", "RR"], ["TMPG"])
                act(FL[:], TMPG[:], AF.Exp, ["TMPG"], ["FL"])
                tt(TMPG[:], RP[:], RR[:], ALU.subtract, ["RP", "RR"], ["TMPG"])
                act(LAM[:], TMPG[:], AF.Exp, ["TMPG"], ["LAM"])
                vcopy(LSEL[0:64, :, :, :], LAM[0:64, :, :, 0:4:2], ["LAM"], ["LSEL"])
                vcopy(LSEL[64:128, :, :, :], LAM[64:128, :, :, 1:4:2], ["LAM"], ["LSEL"])
                dbg("WK", WK[:], shp, ["WK"])
                dbg("FL", FL[:], shp, ["FL"])
                dbg("LAM", LAM[:], shp, ["LAM"])

                chk('c1')
                touched = set()
                mod_steps = {max(1, NT // 4): 0, max(2, NT // 2): 1, max(3, (3 * NT) // 4): 2}
                for i in range(NT):
                    if defer_mod and i in mod_steps:
                        m3_ = mod_steps[i]
                        mod_mm(l + 1, 1024, ["Whi"], 6, m3_)
                        if m3_ < 2:
                            mod_dma(l + 1, 1024, ["Whi"], m3_ + 1)
                        else:
                            mod_fin(l + 1, 6)
                    for dr in range(2):
                        c = orders[dr][i]
                        cs = slice(c * 128, (c + 1) * 128)
                        bS, bX, bD = (0, 1, 2) if dr == 0 else (3, 4, 5)
                        for hh in range(4):
                            hp = 64 * (hh % 2)
                            mm(PS[bS][:, hh * 128:(hh + 1) * 128], kc[:, hh // 2, cs], qc[:, hh, cs], True, True,
                               [("kc", c), ("qc", c)], [pk(bS)])
                        tt(Sm[dr][:], PS[bS][:].rearrange("p (h t) -> p h t", h=4),
                           (maskf if dr == 0 else maskb)[:], ALU.mult,
                           [pk(bS), "maskf", "maskb"], [("Sm", dr)])
                        chk(f"L{i}{dr}a")
                        tt(vt[dr][:, :, 0:65], vC[:, c, :, 0:65], WK[:, dr, c, :].unsqueeze(2).to_broadcast([128, 4, 65]), ALU.mult,
                           [("vC", c), "WK"], [("vt", dr)])
                        chk(f"L{i}{dr}b")
                        if i > 0:
                            tt(Cst[dr][:, :, 0:65], Cst[dr][:, :, 0:65], LSEL[:, dr, c, :].unsqueeze(2).to_broadcast([128, 2, 65]), ALU.mult,
                               [("Cst", dr), "LSEL"], [("Cst", dr)])
                            SC.op("act", lambda h, o=Cbf[dr][:, :, 0:65], i_=Cst[dr][:, :, 0:65]: h.copy(out=o, in_=i_),
                                  reads=[("Cst", dr)], writes=[("Cbf", dr)])
                        for hh in range(4):
                            hp = 64 * (hh % 2)
                            mm(PS[bX][:, hh * 80:hh * 80 + 65], Sm[dr][:, hh, :], vt[dr][:, hh, 0:65], True, i == 0,
                               [("Sm", dr), ("vt", dr)], [pk(bX)])
                            if i > 0:
                                mm(PS[bX][:, hh * 80:hh * 80 + 65], qc[:, hh, cs], Cbf[dr][:, hh // 2, 0:65],
                                   False, True, [("qc", c), ("Cbf", dr)], [pk(bX)])
                        chk(f"L{i}{dr}c")
                        for hh in range(4):
                            hp = 64 * (hh % 2)
                            mm(PS[bD][hp:hp + 64, (hh // 2) * 80:(hh // 2) * 80 + 65], ktok[:, c, hh * 64:(hh + 1) * 64],
                               vt[dr][:, hh, 0:65], True, True, [("kt", c), ("vt", dr)], [pk(bD)])
                        chk(f"L{i}{dr}d")
                        pd = PS[bD][:, 0:160].rearrange("p (j e) -> p j e", j=2)[:, :, 0:65]
                        if i == 0:
                            vcopy(Cst[dr][:, :, 0:65], pd, [pk(bD)], [("Cst", dr)])
                        else:
                            tt(Cst[dr][:, :, 0:65], Cst[dr][:, :, 0:65], pd, ALU.add, [("Cst", dr), pk(bD)], [("Cst", dr)])
                        chk(f"L{i}{dr}e")
                        px = PS[bX][:, 0:320].rearrange("p (h e) -> p h e", h=4)
                        stt(dmx[dr][:, 0:4], px[:, :, 64], -1.0, FL[:, dr, c, :], ALU.mult, ALU.max, [pk(bX), "FL"], [("dmx", dr)])
                        tt(dmx[dr][:, 0:4], dmx[dr][:, 0:4], px[:, :, 64], ALU.max, [pk(bX), ("dmx", dr)], [("dmx", dr)])
                        recip(dmx[dr][:, 4:8], dmx[dr][:, 0:4], [("dmx", dr)], [("dmx", dr)])
                        h3 = hsum[:, c, :].rearrange("p (h d) -> p h d", h=4)
                        if c not in touched:
                            touched.add(c)
                            tt(h3, px[:, :, 0:64], dmx[dr][:, 4:8].unsqueeze(2).to_broadcast([128, 4, 64]), ALU.mult,
                               [pk(bX), ("dmx", dr)], [("hs", c)])
                        else:
                            tt(tmpx[dr][:], px[:, :, 0:64], dmx[dr][:, 4:8].unsqueeze(2).to_broadcast([128, 4, 64]), ALU.mult,
                               [pk(bX), ("dmx", dr)], [("tmpx", dr)])
                            tt(h3, h3, tmpx[dr][:], ALU.add, [("hs", c), ("tmpx", dr)], [("hs", c)])
                dbg("hsum", hsum[:], [128, NT, 256], [("hs", c) for c in range(NT)])
                chk('c2')
                ui = units.index((b, l))
                if ui + 1 < len(units):
                    load_w1(units[ui + 1][1], "hi")
                    w1hi_ready["u"] = units[ui + 1]
                for c0_ in range(0, NT, 2):
                    cs_ = [c for c in (c0_, c0_ + 1) if c < NT]
                    m_ = len(cs_)
                    hc2 = hsum[:, c0_:c0_ + m_, :]
                    hkeys = [("hs", c) for c in cs_]
                    tt(hq[:, 0:m_, :], hc2, hc2, ALU.mult, hkeys, ["hq"])
                    reduce(hst[:, 0:4 * m_], hq[:, 0:m_, :].rearrange("p a (h d) -> p (a h) d", h=4), ALU.add, ["hq"], ["hst"])
                    act(hst[:, 8:8 + 4 * m_], hst[:, 0:4 * m_], AF.Ln, ["hst", "cst"], ["hst"], bias=eps_ap(), scale=1.0 / 64)
                    act(hst[:, 0:4 * m_], hst[:, 8:8 + 4 * m_], AF.Exp, ["hst"], ["hst"], scale=-0.5)
                    tt(hq[:, 0:m_, :].rearrange("p a (h d) -> p (a h) d", h=4), hc2.rearrange("p a (h d) -> p (a h) d", h=4),
                       hst[:, 0:4 * m_].unsqueeze(2).to_broadcast([128, 4 * m_, 64]), ALU.mult, hkeys + ["hst"], ["hq"])
                    for a_, c in enumerate(cs_):
                        tt(hq[:, a_, :], hq[:, a_, :], ghd[:], ALU.mult, ["hq", "ghd"], ["hq"])
                    tt(yct[:, 0:m_, :], hq[:, 0:m_, :], og[:, c0_:c0_ + m_, :], ALU.mult, ["hq"] + [("og", c) for c in cs_], ["yct"])
                    for a_, c in enumerate(cs_):
                        for j in range(2):
                            tr(PSB[:, j * 512 + (c % 4) * 128:j * 512 + (c % 4 + 1) * 128], yct[:, a_, j * 128:(j + 1) * 128], identb[:],
                               ["yct", "identb"], ["psbf"])
                        if c % 4 == 3 or c == NT - 1:
                            cb = (c // 4) * 4
                            w = (c - cb + 1) * 128
                            SC.op("act", lambda h, o=yT[:, 6:8, cb * 128:cb * 128 + w], i_=PSB[:].rearrange("p (j t) -> p j t", j=2)[:, :, 0:w]:
                                  h.copy(out=o, in_=i_), reads=["psbf"], writes=[("yTc", t) for t in range(cb, c + 1)])
                dbg("yT_c", yT[:], [128, 8, T], [("yTc", t) for t in range(NT)])
                SC.barrier()
                A.release(um)

                chk('c')
                xb1 = A("xb1", [128, KC, 512], F32)
                allkeys = ([("yTa", t) for t in range(NT)] + [("yTb", t, kv) for t in range(NT) for kv in (0, 1)]
                           + [("yTc", t) for t in range(NT)])
                def oload(bi_):
                    t0_, n_ = cfg.blocks[bi_]
                    if bi_ % 2 == 0:
                        xload(b, bi_)
                        xstate["loaded"] = None
                    else:
                        dma("pool", xb1[:, :, 0:n_], xT_d[b, :, t0_:t0_ + n_].rearrange("(k p) t -> p k t", p=128), "xldo",
                            reads=[("xd", b, bi_)], writes=["xb1"])

                oload(0)
                ocnt_ = 0
                for bi, (t0, n) in enumerate(cfg.blocks):
                    r = NB if t0 < CTX else b
                    xb = xb0 if bi % 2 == 0 else xb1
                    xk = "xb0" if bi % 2 == 0 else "xb1"
                    tiles = [t0 // 128 + i for i in range(n // 128)]
                    ykeys = ([("yTa", t) for t in tiles] + [("yTb", t, kv) for t in tiles for kv in (0, 1)] + [("yTc", t) for t in tiles])
                    if bi + 1 < len(cfg.blocks):
                        oload(bi + 1)
                    for m in range(8):
                        bank = ocnt_ % 4
                        ocnt_ += 1
                        for k in range(KC):
                            mm(PS[bank][:, 0:n], Wreg[:, k, m * 128:(m + 1) * 128], yT[:, k, t0:t0 + n], k == 0, k == KC - 1,
                               ["Wlo_a", "Wlo_q"] + ykeys, [pk(bank)])
                        stt(xb[:, m, 0:n], PS[bank][:, 0:n], modt[:, l, 16 + m, r:r + 1], xb[:, m, 0:n], ALU.mult, ALU.add,
                            [pk(bank), ("modt", l), xk], [xk])
                    dma("sp", xT_d[b, :, t0:t0 + n].rearrange("(k p) t -> p k t", p=128), xb[:, :, 0:n], "xst",
                        reads=[xk], writes=[("xd", b, bi)])
                    if bi == 0 and l + 1 < DEPTH:
                        norm_block(b, l + 1, 0, nxt=None, slot=0, preloaded=True)
                        early["p1"] = (b, l + 1)
                SC.barrier()
                A.release(um)

            chk('o')
            fm = A.mark()
            xbf1 = A("xbf1", [128, KC, 512], F32)
            otile = [A(f"otile{i}", [128, D], F32) for i in range(3)]
            latb = [(bi, t0, n) for bi, (t0, n) in enumerate(cfg.blocks) if t0 >= CTX]
            xstate["loaded"] = None

            def fload(idx):
                bi_, t0_, n_ = latb[idx]
                buf_ = xb0 if idx % 2 == 0 else xbf1
                key_ = "xb0" if idx % 2 == 0 else "xbf1"
                dma("pool", buf_[:, :, 0:n_], xT_d[b, :, t0_:t0_ + n_].rearrange("(k p) t -> p k t", p=128), "xldf",
                    reads=[("xd", b, bi_)], writes=[key_])

            fload(0)
            oc = 0
            for idx, (bi, t0, n) in enumerate(latb):
                if idx + 1 < len(latb):
                    fload(idx + 1)
                xbf = xb0 if idx % 2 == 0 else xbf1
                xk = "xb0" if idx % 2 == 0 else "xbf1"
                for k in range(KC):
                    sk = ("sqk", k % 2)
                    act(sqk[k % 2][:, 0:n], xbf[:, k, 0:n], AF.Square, [xk], [sk])
                    mm(PS[6][:, 0:n], ones_b[:], sqk[k % 2][:, 0:n], k == 0, k == KC - 1, [sk, "ones_b"], [pk(6)])
                act(rs_t[:, 0:n], PS[6][:, 0:n], AF.Ln, [pk(6), "cst"], ["rs_t"], bias=eps_ap(), scale=1.0 / D)
                act(rstd[:, 0:n], rs_t[:, 0:n], AF.Exp, ["rs_t"], ["rstd"], scale=-0.5)
                for k in range(KC):
                    stt(xbf[:, k, 0:n], xbf[:, k, 0:n], gfin[:, k:k + 1], rstd[:, 0:n], ALU.mult, ALU.mult,
                        [xk, "gfin", "rstd"], [xk])
                for ti in range(n // 128):
                    ot = otile[oc % 3]
                    okey = ("otile", oc % 3)
                    bA, bB = (1, 2) if oc % 2 == 0 else (3, 4)
                    oc += 1
                    for k in range(KC):
                        bank = bA if k < 4 else bB
                        tr(PS[bank][:, (k % 4) * 128:(k % 4 + 1) * 128], xbf[:, k, ti * 128:(ti + 1) * 128], ident[:],
                           [xk, "ident"], [pk(bank)])
                    vcopy(ot[:, 0:512], PS[bA][:], [pk(bA)], [okey])
                    SC.op("act", lambda h, o=ot[:, 512:1024], i_=PS[bB][:]: h.copy(out=o, in_=i_), reads=[pk(bB)], writes=[okey])
                    s0 = t0 - CTX + ti * 128
                    dma("sp", out_d[b, s0:s0 + 128, :], ot[:], "ost", reads=[okey], writes=[("out", b, s0)])
            A.release(fm)
            SC.barrier()

    except _Stop:
        pass
    SC.barrier()
    nc._sbuf_peak = A.peak
    nc._nops = {e: len(SC.ops[e]) for e in ENGS}
    fns = SC.emit()
    with nc.Block() as block:
        block.sync(fns["sp"])
        block.scalar(fns["act"])
        block.vector(fns["dve"])
        block.gpsimd(fns["pool"])
        block.tensor(fns["pe"])
    return nc


def make_in_maps(cfg, n_cores, x, c, ctx, c_ctx, w_ada, b_ada, g_norm, w_in, w_s, b_s, g_q, g_k, b_gates, g_head,
                 w_out, g_final):
    f = lambda a: np.ascontiguousarray(np.asarray(a, dtype=np.float32))
    DEPTH, NB, NR = cfg.DEPTH, cfg.NB, cfg.NR
    consts = _consts(cfg)
    shared = dict(consts)
    shared["w_ada"] = f(w_ada)
    shared["w_in"] = f(w_in)
    shared["w_s"] = f(w_s)
    shared["w_out"] = f(w_out)
    shared["b_adaT"] = np.stack([_vecT(f(b_ada)[l], 24) for l in range(DEPTH)])
    shared["g_normT"] = np.stack([_vecT(f(g_norm)[l], KC) for l in range(DEPTH)])
    shared["g_finalT"] = _vecT(f(g_final), KC)
    bs = f(b_s)
    bsT = np.zeros((DEPTH, 128, 256), np.float32)
    for l in range(DEPTH):
        for j in range(2):
            for i in range(2):
                bsT[l, 64 * i:64 * i + 64, j * 128:(j + 1) * 128] = bs[l, 2 * j + i][None, :]
    shared["bsT"] = bsT
    shared["gq"] = np.ascontiguousarray(np.tile(f(g_q), (1, 2))[:, :, None])
    shared["gk"] = np.ascontiguousarray(np.tile(f(g_k), (1, 2))[:, :, None])
    shared["bgate"] = np.ascontiguousarray(np.broadcast_to(f(b_gates)[:, None, :], (DEPTH, 128, 16)))
    shared["ghead"] = np.ascontiguousarray(np.broadcast_to(f(g_head)[:, None, :], (DEPTH, 128, 256)))
    x = f(x)
    c = f(c)
    ctx = f(ctx)
    c_ctx = f(c_ctx)
    maps = []
    for i in range(n_cores):
        sl = slice(i * NB, (i + 1) * NB)
        rows = np.concatenate([c[sl], c_ctx[None, :]], axis=0)
        cT = np.ascontiguousarray(rows.reshape(NR, KC, 128).transpose(2, 1, 0).reshape(128, KC * NR))
        m = dict(shared)
        m["x"] = np.ascontiguousarray(x[sl])
        m["ctx"] = np.ascontiguousarray(ctx[sl])
        m["cT"] = cT
        maps.append(m)
    return maps


_NC_CACHE = {}


def kernel(x, c, ctx, c_ctx, w_ada, b_ada, g_norm, w_in, w_s, b_s, g_q, g_k, b_gates, g_head, w_out, g_final):
    n_cores = 8
    B, S, _ = np.asarray(x).shape
    CTX = np.asarray(ctx).shape[1]
    DEPTH = np.asarray(w_in).shape[0]
    cfg = Cfg(S=S, CTX=CTX, DEPTH=DEPTH, NB=B // n_cores)
    key = (S, CTX, DEPTH, cfg.NB)
    if key not in _NC_CACHE:
        _NC_CACHE[key] = build(cfg)
    nc = _NC_CACHE[key]
    maps = make_in_maps(cfg, n_cores, x, c, ctx, c_ctx, w_ada, b_ada, g_norm, w_in, w_s, b_s, g_q, g_k, b_gates,
                        g_head, w_out, g_final)
    res = run_bass_kernel_spmd(nc, maps, core_ids=list(range(n_cores)))
    return np.concatenate([r["out"] for r in res.results], axis=0).astype(np.float32)
```

```python
import numpy as np
import concourse.bass as bass
import concourse.mybir as mybir
from concourse.bass_utils import run_bass_kernel_spmd

F32 = mybir.dt.float32
BF16 = mybir.dt.bfloat16
AF = mybir.ActivationFunctionType
ALU = mybir.AluOpType
AX = mybir.AxisListType

D = 1024
KC = 8
DIN = 3344
EPS = 1e-6
ENGS = ("pe", "act", "dve", "pool", "sp")
SEM_CHUNK = 8000


class _Op:
    __slots__ = ("eng", "fn", "deps", "marked", "rank", "dma", "dgroup")

    def __init__(self, eng, fn):
        self.eng = eng
        self.fn = fn
        self.deps = []
        self.marked = False
        self.rank = None
        self.dma = None
        self.dgroup = None


class _DmaSem:
    def __init__(self, name):
        self.name = name
        self.count = 0
        self.last = None


class Sched:
    def __init__(self, nc):
        self.nc = nc
        self.ops = {e: [] for e in ENGS}
        self.last_writer = {}
        self.readers = {}
        self.dma_sems = {}

    def _dep(self, op, prod):
        if prod is None or prod is op:
            return
        if prod.eng == op.eng and prod.dma is None and op.dma is None and op.eng == "pe":
            return
        snap = 0
        if prod.dma is not None:
            snap = prod.dma.count - (1 if op.dma is prod.dma else 0)
        op.deps.append((prod, snap))

    def op(self, eng, fn, reads=(), writes=(), dma=None):
        o = _Op(eng, fn)
        if dma is not None:
            ds = self.dma_sems.get(dma)
            if ds is None:
                ds = self.dma_sems[dma] = _DmaSem(dma)
            ds.count += 1
            ds.last = o
            o.dma = ds
            o.dgroup = ds.count
        self.ops[eng].append(o)
        psr = [k for k in reads if isinstance(k, tuple) and k[0] == "ps"]
        if psr:
            reads = [k for k in reads if k not in psr]
            writes = list(writes) + [k for k in psr if k not in writes]
        for k in reads:
            w = self.last_writer.get(k)
            if w is not None:
                self._dep(o, w)
            self.readers.setdefault(k, []).append(o)
        for k in writes:
            w = self.last_writer.get(k)
            if w is not None:
                self._dep(o, w)
            for r in self.readers.get(k, ()):
                if r is o:
                    continue
                self._dep(o, r)
            self.last_writer[k] = o
            self.readers[k] = []
        return o

    def barrier(self):
        lasts = []
        for e in ENGS:
            for o_ in reversed(self.ops[e]):
                if not getattr(o_.fn, "_is_nop", False):
                    lasts.append(o_)
                    break
        dl = [ds.last for ds in self.dma_sems.values() if ds.last is not None]
        for e in ENGS:
            fn = lambda h: h.nop()
            fn._is_nop = True
            o = _Op(e, fn)
            self.ops[e].append(o)
            for p in lasts + dl:
                if p.eng == e and p.dma is None:
                    continue
                o.deps.append((p, p.dma.count if p.dma is not None else 0))

    def emit(self):
        nc = self.nc
        for e in ENGS:
            for o in self.ops[e]:
                for p, _ in o.deps:
                    p.marked = True
        eng_sems = {}
        for e in ENGS:
            r = 0
            for o in self.ops[e]:
                if o.dma is None and o.marked:
                    o.rank = r
                    r += 1
            eng_sems[e] = [nc.alloc_semaphore(f"prog_{e}_{i}") for i in range(max((r + SEM_CHUNK - 1) // SEM_CHUNK, 1))]
        dsem = {name: nc.alloc_semaphore(f"dma_{name}") for name in self.dma_sems}

        def make(e):
            def run(h):
                waited = {}
                for o in self.ops[e]:
                    need = {}
                    for p, snap in o.deps:
                        if p.dma is not None:
                            base = ("d", p.dma.name)
                            tot = max(p.dgroup, snap) * 16
                            sem, val = dsem[p.dma.name], tot
                        else:
                            base = ("e", p.eng)
                            tot = p.rank + 1
                            sem, val = eng_sems[p.eng][p.rank // SEM_CHUNK], (p.rank % SEM_CHUNK) + 1
                        if waited.get(base, 0) >= tot:
                            continue
                        if need.get(base, (0,))[0] < tot:
                            need[base] = (tot, sem, val)
                    for base, (tot, sem, val) in need.items():
                        h.wait_ge(sem, val)
                        waited[base] = tot
                    ins = o.fn(h)
                    if o.dma is not None:
                        ins.then_inc(dsem[o.dma.name], 16)
                    elif o.marked:
                        ins.then_inc(eng_sems[e][o.rank // SEM_CHUNK], 1)
            return run

        return {e: make(e) for e in ENGS}


class Alloc:
    def __init__(self, nc, base=16512, limit=229376 - 64):
        self.nc = nc
        self.off = base
        self.limit = limit
        self.n = 0

    def __call__(self, name, shape, dtype):
        sz = int(np.prod(shape[1:])) * (4 if dtype == F32 else 2)
        off = (self.off + 31) // 32 * 32
        assert off + sz <= self.limit, f"SBUF overflow at {name}: {off}+{sz} > {self.limit}"
        self.off = off + sz
        self.peak = max(getattr(self, "peak", 0), self.off)
        self.n += 1
        return self.nc.alloc_sbuf_tensor_at(f"{name}_{self.n}", list(shape), dtype, offset=off)

    def mark(self):
        return self.off

    def release(self, m):
        self.off = m


class _Stop(Exception):
    pass


class Cfg:
    def __init__(self, S=2048, CTX=256, DEPTH=4, NB=2, debug=(), stop=None):
        self.S, self.CTX, self.DEPTH, self.NB = S, CTX, DEPTH, NB
        self.T = S + CTX
        self.NT = self.T // 128
        self.NCX = CTX // 128
        self.NR = NB + 1
        self.debug = tuple(debug)
        self.stop = stop
        blocks = []
        for (st, ln) in ((0, CTX), (CTX, S)):
            o = 0
            while o < ln:
                n = min(512, ln - o)
                blocks.append((st + o, n))
                o += n
        self.blocks = blocks


def _consts(cfg):
    S = cfg.S
    c = {}
    c["ident"] = np.eye(128, dtype=np.float32)
    tri = (np.arange(128)[:, None] <= np.arange(128)[None, :]).astype(np.float32)
    c["trif"] = tri
    c["trib"] = np.ascontiguousarray(tri.T)
    bo = np.zeros((128, 128), np.float32)
    bo[:64, :64] = 1
    bo[64:, 64:] = 1
    c["blockones"] = bo
    rot = np.zeros((128, 128), np.float32)
    for i in range(128):
        if (i % 32) < 16:
            rot[i + 16, i] = -1.0
        else:
            rot[i - 16, i] = 1.0
    c["rot"] = rot
    freqs = (10000.0 ** (-np.arange(16, dtype=np.float32) / 16)).astype(np.float32)
    s = np.arange(S)
    row = (s // 64).astype(np.float32)
    col = (s % 64).astype(np.float32)
    ang = np.zeros((128, S), np.float32)
    for p in range(128):
        d = p % 64
        if d < 32:
            ang[p] = row * freqs[d % 16]
        else:
            ang[p] = col * freqs[(d - 32) % 16]
    c["cosT"] = np.cos(ang).astype(np.float32)
    c["sinT"] = np.sin(ang).astype(np.float32)
    return c


def _vecT(v, nch):
    return np.ascontiguousarray(v.reshape(nch, 128).T)


def build(cfg):
    S_, CTX, DEPTH, NB, T, NT, NCX, NR = cfg.S, cfg.CTX, cfg.DEPTH, cfg.NB, cfg.T, cfg.NT, cfg.NCX, cfg.NR
    nc = bass.Bass("TRN2", target_bir_lowering=False)
    SC = Sched(nc)
    A = Alloc(nc)

    def din(name, shape):
        return nc.dram_tensor(name, list(shape), F32, kind="ExternalInput").ap()

    x_d = din("x", [NB, S_, D])
    ctx_d = din("ctx", [NB, CTX, D])
    cT_d = din("cT", [128, KC * NR])
    wada_d = din("w_ada", [DEPTH, D, 3 * D])
    bada_d = din("b_adaT", [DEPTH, 128, 24])
    gnorm_d = din("g_normT", [DEPTH, 128, KC])
    win_d = din("w_in", [DEPTH, D, DIN])
    ws_d = din("w_s", [DEPTH, 4, 128, 128])
    bsT_d = din("bsT", [DEPTH, 128, 256])
    gq_d = din("gq", [DEPTH, 128, 1])
    gk_d = din("gk", [DEPTH, 128, 1])
    bg_d = din("bgate", [DEPTH, 128, 16])
    gh_d = din("ghead", [DEPTH, 128, 256])
    wout_d = din("w_out", [DEPTH, D, D])
    gfin_d = din("g_finalT", [128, KC])
    ident_d = din("ident", [128, 128])
    trif_d = din("trif", [128, 128])
    trib_d = din("trib", [128, 128])
    bones_d = din("blockones", [128, 128])
    rot_d = din("rot", [128, 128])
    cos_d = din("cosT", [128, S_])
    sin_d = din("sinT", [128, S_])
    out_d = nc.dram_tensor("out", [NB, S_, D], F32, kind="ExternalOutput").ap()
    xT_d = nc.dram_tensor("xT_scratch", [NB, D, T], F32, kind="Internal").ap()
    dbg_d = {}

    PSALL = nc.alloc_psum_tensor("psall", [128, 7 * 512], F32)
    PS = [PSALL[:, i * 512:(i + 1) * 512] for i in range(7)]
    PSB = nc.alloc_psum_tensor("psbf", [128, 1024], BF16)

    def pk(i):
        return ("ps", i)

    ident = A("ident", [128, 128], F32)
    trif = A("trif", [128, 128], F32)
    trib = A("trib", [128, 128], F32)
    maskf = A("maskf", [128, 4, 128], BF16)
    maskb = A("maskb", [128, 4, 128], BF16)
    identb = A("identb", [128, 128], BF16)
    bones = A("bones", [128, 128], BF16)
    rotb = A("rotb", [128, 128], BF16)
    ones_b = A("ones_b", [128, 128], BF16)
    ones_f = A("ones_f", [128, 128], F32)
    cst = A("cst", [128, 4], F32)
    zero4 = A("zero4", [128, 4], F32)
    modt = A("modt", [128, DEPTH, 24, NR], F32)
    gmod = A("gmod", [128, DEPTH, KC, NR], F32)
    gnorm = A("gnorm", [128, DEPTH, KC], F32)
    bada = A("bada", [128, DEPTH, 24], F32)
    gfin = A("gfin", [128, KC], F32)
    gqk = A("gqk", [128, DEPTH, 2], F32)
    yT = A("yT", [128, 8, T], BF16)
    xb0 = A("xb0", [128, KC, 512], F32)
    xn = A("xn", [128, KC, 512], BF16)
    xn2_own = A("xn2", [128, KC, 512], BF16) if 2 * T < KC * 512 else None
    tmpk = [A(f"tmpk{i}", [128, 512], F32) for i in range(2)]
    sqk = [A(f"sqk{i}", [128, 512], BF16) for i in range(2)]
    rs_t = A("rs_t", [128, 512], F32)
    rstd = A("rstd", [128, 512], F32)
    WR_mark0 = A.mark()
    Wreg = A("Wreg", [128, KC, 2048], BF16)
    BC0 = A.mark()

    def eps_ap(n=128):
        return cst[0:n, 0:1]

    def one_ap(n=128):
        return cst[0:n, 1:2]

    def dma(q, out, in_, sem, reads=(), writes=()):
        SC.op(q, lambda h: h.dma_start(out=out, in_=in_), reads=reads, writes=writes, dma=sem)

    def act(out, in_, func, reads, writes, bias=None, scale=None):
        kw = {}
        if bias is not None:
            kw["bias"] = bias
        if scale is not None:
            kw["scale"] = scale
        SC.op("act", lambda h: h.activation(out=out, in_=in_, func=func, **kw), reads=reads, writes=writes)

    def tt(out, in0, in1, op, reads, writes, eng="dve"):
        SC.op(eng, lambda h: h.tensor_tensor(out=out, in0=in0, in1=in1, op=op), reads=reads, writes=writes)

    def ts(out, in0, s1, s2, op0, op1, reads, writes):
        if op1 is None:
            SC.op("dve", lambda h: h.tensor_scalar(out=out, in0=in0, scalar1=s1, scalar2=None, op0=op0), reads=reads, writes=writes)
        else:
            SC.op("dve", lambda h: h.tensor_scalar(out=out, in0=in0, scalar1=s1, scalar2=s2, op0=op0, op1=op1), reads=reads, writes=writes)

    def stt(out, in0, scalar, in1, op0, op1, reads, writes):
        SC.op("dve", lambda h: h.scalar_tensor_tensor(out=out, in0=in0, scalar=scalar, in1=in1, op0=op0, op1=op1),
              reads=reads, writes=writes)

    def vcopy(out, in_, reads, writes, eng="dve"):
        SC.op(eng, lambda h: h.tensor_copy(out=out, in_=in_), reads=reads, writes=writes)

    def recip(out, in_, reads, writes):
        SC.op("dve", lambda h: h.reciprocal(out=out, in_=in_), reads=reads, writes=writes)

    def memset(ap, val, writes):
        SC.op("dve", lambda h: h.memset(ap, val), writes=writes)

    def mm(out, lhsT, rhs, start, stop, reads, writes):
        SC.op("pe", lambda h: h.matmul(out, lhsT=lhsT, rhs=rhs, start=start, stop=stop), reads=reads, writes=writes)

    def tr(out, in_, idn, reads, writes):
        SC.op("pe", lambda h: h.transpose(out, in_, idn), reads=reads, writes=writes)

    def reduce(out, in_, op, reads, writes):
        SC.op("dve", lambda h: h.tensor_reduce(out=out, in_=in_, axis=AX.X, op=op), reads=reads, writes=writes)

    def chk(name):
        if cfg.stop == name:
            raise _Stop()

    def dbg(name, src_ap, shape, reads):
        if name in cfg.debug:
            if name not in dbg_d:
                dbg_d[name] = nc.dram_tensor("dbg_" + name, list(shape), src_ap.dtype, kind="ExternalOutput").ap()
            dma("sp", dbg_d[name], src_ap, "dbg", reads=reads, writes=[("dbg", name)])

    try:
        dma("sp", ident[:], ident_d, "c0", writes=["ident"])
        dma("sp", trif[:], trif_d, "c0", writes=["trif"])
        dma("sp", trib[:], trib_d, "c0", writes=["trib"])
        for hh in range(4):
            dma("pool", maskf[:, hh, :], trif_d, "c1", writes=["maskf"])
            dma("pool", maskb[:, hh, :], trib_d, "c1", writes=["maskb"])
        dma("pool", identb[:], ident_d, "c1", writes=["identb"])
        dma("pool", bones[:], bones_d, "c1", writes=["bones"])
        dma("pool", rotb[:], rot_d, "c1", writes=["rotb"])
        dma("sp", gnorm[:], gnorm_d.rearrange("l p k -> p l k"), "c0", writes=["gnorm"])
        dma("sp", bada[:], bada_d.rearrange("l p k -> p l k"), "c0", writes=["bada"])
        dma("sp", gfin[:], gfin_d, "c0", writes=["gfin"])
        for l in range(DEPTH):
            dma("sp", gqk[:, l, 0:1], gq_d[l], "c0", writes=["gqk"])
            dma("sp", gqk[:, l, 1:2], gk_d[l], "c0", writes=["gqk"])
        memset(ones_b[:], 1.0, ["ones_b"])
        memset(ones_f[:], 1.0, ["ones_f"])
        memset(cst[:, 0:1], EPS, ["cst"])
        memset(cst[:, 1:2], 1.0, ["cst"])
        memset(cst[:, 2:4], 0.0, ["cst"])
        memset(zero4[:], 0.0, ["zero4"])

        chk('consts')
        m0 = A.mark()
        cTs = A("cTs", [128, KC, NR], F32)
        csb = A("csb", [128, KC, NR], BF16)
        dma("sp", cTs[:].rearrange("p k r -> p (k r)"), cT_d, "c0", writes=["cTs"])
        act(csb[:], cTs[:], AF.Silu, ["cTs"], ["csb"])
        wada = Wreg

        def mod_dma(l, base, wkeys, m3):
            dma("pool", wada[:, :, base:base + 1024], wada_d[l, :, m3 * 1024:(m3 + 1) * 1024].rearrange("(k p) n -> p k n", p=128),
                "wada", writes=wkeys)

        def mod_mm(l, base, wkeys, bank, m3):
            for j in range(8):
                ch = m3 * 8 + j
                for k in range(KC):
                    mm(PS[bank][:, ch * NR:(ch + 1) * NR], wada[:, k, base + j * 128:base + (j + 1) * 128], csb[:, k, :],
                       k == 0, k == KC - 1, wkeys + ["csb"], [pk(bank)])

        def mod_layer(l, base, wkeys, bank):
            for m3 in range(3):
                mod_dma(l, base, wkeys, m3)
                mod_mm(l, base, wkeys, bank, m3)
            mod_fin(l, bank)

        def mod_fin(l, bank):
            tt(modt[:, l, :, :], PS[bank][:, 0:24 * NR].rearrange("p (c r) -> p c r", r=NR),
               bada[:, l, :].unsqueeze(2).to_broadcast([128, 24, NR]), ALU.add, [pk(bank), "bada"], [("modt", l)])
            ts(gmod[:, l, :, :], modt[:, l, 8:16, :], 1.0, None, ALU.add, None, [("modt", l)], [("gmod", l)])
            tt(gmod[:, l, :, :], gmod[:, l, :, :], gnorm[:, l, :].unsqueeze(2).to_broadcast([128, KC, NR]), ALU.mult,
               [("gmod", l), "gnorm"], [("gmod", l)])


        chk('mod')
        m0 = A.mark()
        intile = [A(f"intile{i}", [128, D], F32) for i in range(2)]
        mod_dma(0, 0, ["Wlo_a", "Wlo_q"], 0)
        cnt = 0
        for b in range(NB):
            for bi, (t0, n) in enumerate(cfg.blocks):
                for ti in range(n // 128):
                    tok = t0 + ti * 128
                    src = ctx_d[b, tok:tok + 128, :] if tok < CTX else x_d[b, tok - CTX:tok - CTX + 128, :]
                    it = intile[cnt % 2]
                    ik = ("intile", cnt % 2)
                    dma("pool", it[:], src, f"in{cnt % 2}", writes=[ik])
                    for k in range(KC):
                        bank = 1 + (k // 4)
                        tr(PS[bank][:, (k % 4) * 128:(k % 4 + 1) * 128], it[:, k * 128:(k + 1) * 128], ident[:],
                           [ik, "ident"], [pk(bank)])
                    vcopy(xb0[:, 0:4, ti * 128:(ti + 1) * 128], PS[1][:].rearrange("p (k t) -> p k t", k=4), [pk(1)], ["xb0_lo"])
                    SC.op("act", lambda h, o=xb0[:, 4:8, ti * 128:(ti + 1) * 128], i=PS[2][:].rearrange("p (k t) -> p k t", k=4):
                          h.copy(out=o, in_=i), reads=[pk(2)], writes=["xb0_hi"])
                    cnt += 1
                dma("sp", xT_d[b, :, t0:t0 + n].rearrange("(k p) t -> p k t", p=128), xb0[:, :, 0:n], "xst",
                    reads=["xb0_lo", "xb0_hi"], writes=[("xd", b, bi)])
        A.release(m0)
        mod_mm(0, 0, ["Wlo_a", "Wlo_q"], 0, 0)
        for m3_ in (1, 2):
            mod_dma(0, 0, ["Wlo_a", "Wlo_q"], m3_)
            mod_mm(0, 0, ["Wlo_a", "Wlo_q"], 0, m3_)
        mod_fin(0, 0)
        SC.barrier()

        chk('setup')
        xstate = {"loaded": None}

        def xload(b, bi):
            if xstate["loaded"] == (b, bi):
                return
            t0_, n_ = cfg.blocks[bi]
            dma("sp", xb0[:, :, 0:n_], xT_d[b, :, t0_:t0_ + n_].rearrange("(k p) t -> p k t", p=128), "xld",
                reads=[("xd", b, bi)], writes=["xb0"])
            xstate["loaded"] = (b, bi)

        if 2 * T >= KC * 512:
            xn2 = yT[:, 6:8, :].rearrange("p a t -> p (a t)")[:, 0:KC * 512].rearrange("p (k n) -> p k n", k=KC)
        else:
            xn2 = xn2_own[:]
        xnbuf = [xn[:], xn2]

        early = {"p1": None}

        def norm_block(b, l, bi, final=False, nxt=None, slot=0, preloaded=False):
            t0, n = cfg.blocks[bi]
            r = NB if t0 < CTX else b
            if not preloaded:
                xload(b, bi)
            for k in range(KC):
                sk = ("sqk", k % 2)
                act(sqk[k % 2][:, 0:n], xb0[:, k, 0:n], AF.Square, ["xb0"], [sk])
                mm(PS[6][:, 0:n], ones_b[:], sqk[k % 2][:, 0:n], k == 0, k == KC - 1, [sk, "ones_b"], [pk(6)])
            act(rs_t[:, 0:n], PS[6][:, 0:n], AF.Ln, [pk(6), "cst"], ["rs_t"], bias=eps_ap(), scale=1.0 / D)
            act(rstd[:, 0:n], rs_t[:, 0:n], AF.Exp, ["rs_t"], ["rstd"], scale=-0.5)
            if not final:
                for k in range(KC):
                    tk = ("tmpk", k % 2)
                    XE = "dve" if bi == 0 else "pool"
                    tt(tmpk[k % 2][:, 0:n], xb0[:, k, 0:n], rstd[:, 0:n], ALU.mult, ["xb0", "rstd"], [tk], eng=XE)
                    SC.op(XE, lambda h, o=xnbuf[slot][:, k, 0:n], i_=tmpk[k % 2][:, 0:n], s1=gmod[:, l, k, r:r + 1], s2=modt[:, l, k, r:r + 1]:
                          h.tensor_scalar(out=o, in0=i_, scalar1=s1, scalar2=s2, op0=ALU.mult, op1=ALU.add),
                          reads=[tk, ("gmod", l), ("modt", l)], writes=[("xn", slot)])
                if nxt is not None:
                    xload(b, nxt)
            return r, t0, n

        def wkey(col):
            return "Wlo_a" if col < 512 else ("Wlo_q" if col < 1024 else "Whi")

        def load_w_cols(dst_col, src_c0, ncols, l, sem):
            keys = sorted(set([wkey(dst_col), wkey(dst_col + ncols - 1)]))
            dma("pool", Wreg[:, :, dst_col:dst_col + ncols],
                win_d[l, :, src_c0:src_c0 + ncols].rearrange("(k p) n -> p k n", p=128), "W_" + keys[0], writes=keys)

        def load_w1(l, part):
            if part == "lo":
                load_w_cols(0, 0, 256, l, "W")
                load_w_cols(256, 512, 256, l, "W")
                for j in range(4):
                    load_w_cols(512 + 128 * j, 768 + 64 * j, 64, l, "W")
                    load_w_cols(512 + 128 * j + 64, 768 + 64 * (4 + j), 64, l, "W")
            else:
                load_w_cols(1024, 1280, 128, l, "W")
                for j in range(4):
                    load_w_cols(1152 + 128 * j, 1536 + 64 * j, 64, l, "W")
                    load_w_cols(1152 + 128 * j + 64, 1536 + 64 * (4 + j), 64, l, "W")
                load_w_cols(1664, 256, 256, l, "W")
                load_w_cols(1920, 1408, 128, l, "W")

        units = [(b_, l_) for b_ in range(NB) for l_ in range(DEPTH)]
        w1hi_ready = {"u": None}

        for b in range(NB):
            for l in range(DEPTH):
                um = A.mark()
                load_w1(l, "lo")
                if w1hi_ready["u"] != (b, l):
                    load_w1(l, "hi")

                qT = A("qT", [128, 4, T], BF16)
                kT = A("kT", [128, 2, T], BF16)
                vB = A("vB", [128, NT, 2, 128], BF16)
                cosT = A("cosT", [128, S_], F32)
                sinT = A("sinT", [128, S_], F32)
                wsraw = A("wsraw", [128, 4, 128], F32)
                wsT = A("wsT", [128, 4, 128], BF16)
                bsT = A("bsT", [128, 256], F32)
                gu = A("gu", [128, 2, 512], BF16)
                vg = A("vg", [128, 4, 256], F32)
                vsq4 = A("vsq4", [128, 4, 256], F32)
                vn = A("vn", [128, 16, 64], BF16)
                st4 = A("st4", [128, 64], F32)
                sqq = A("sqq", [128, 512], BF16)
                rq = A("rq", [128, 512], F32)
                qnb = A("qnb", [128, 512], BF16)
                t1 = A("t1", [128, 512], F32)
                t2 = A("t2", [128, 512], F32)
                gtmp = A("gtmp", [128, 256], F32)
                pTall = A("pTall", [128, 6, 512], BF16)
                pT = [pTall[:, i, :] for i in range(6)]
                vsq_flat = vsq4[:].rearrange("p a b -> p (a b)")
                rden = [vsq_flat[:, 0:512], vsq_flat[:, 512:1024]]
                tmpo = A("tmpo", [128, 512], F32)

                dma("sp", wsraw[:], ws_d[l].rearrange("g p q -> p g q"), "c0", writes=["wsraw"])
                dma("sp", bsT[:], bsT_d[l], "c0", writes=["bsT"])
                dma("sp", cosT[:], cos_d, "cs", writes=["cosT"])
                dma("sp", sinT[:], sin_d, "cs", writes=["sinT"])
                memset(vB[:], 1.0, [("vB", t) for t in range(NT)])
                memset(kT[:], 0.0, [("kT", t) for t in range(NT)])
                for g in range(4):
                    tr(PS[5][:, g * 128:(g + 1) * 128], wsraw[:, g, :], ident[:], ["wsraw", "ident"], [pk(5)])
                vcopy(wsT[:], PS[5][:].rearrange("p (g q) -> p g q", g=4), [pk(5)], ["wsT"])

                nblk = len(cfg.blocks)
                fcnt = [0]
                if early["p1"] == (b, l):
                    xload(b, 1 % nblk)
                else:
                    norm_block(b, l, 0, nxt=1 % nblk, slot=0)
                for bi in range(nblk):
                    t0, n = cfg.blocks[bi]
                    r = NB if t0 < CTX else b
                    xnb = xnbuf[bi % 2]
                    xnk = ("xn", bi % 2)
                    lat = t0 >= CTX
                    s0 = t0 - CTX
                    tiles = [t0 // 128 + i for i in range(n // 128)]
                    nt_ = len(tiles)
                    for ti, tg in enumerate(tiles):
                        c0 = ti * 128
                        tb = 2 + (ti % 2)
                        for k in range(KC):
                            mm(PS[tb][:, 0:384], xnb[:, k, c0:c0 + 128], Wreg[:, k, 1664:2048], k == 0, k == KC - 1,
                               ["Whi", xnk], [pk(tb)])
                        act(vg[:, ti, :], PS[tb][:, 0:256], AF.Gelu_apprx_tanh, [pk(tb)], ["vg"])
                        SC.op("act", lambda h, o=vB[:, tg, 0, 0:64], i=PS[tb][:, 256:320]: h.copy(out=o, in_=i),
                              reads=[pk(tb)], writes=[("vB", tg)])
                        SC.op("act", lambda h, o=vB[:, tg, 1, 64:128], i=PS[tb][:, 320:384]: h.copy(out=o, in_=i),
                              reads=[pk(tb)], writes=[("vB", tg)])
                    fbanks = [0, 1, 5]
                    for ci in [0, 1, 2, 3, 9, 10, 11, 12]:
                        bank = fbanks[fcnt[0] % 3]
                        fcnt[0] += 1
                        for k in range(KC):
                            mm(PS[bank][:, 0:n], Wreg[:, k, ci * 128:(ci + 1) * 128], xnb[:, k, 0:n], k == 0, k == KC - 1,
                               [wkey(ci * 128), xnk], [pk(bank)])
                        src = PS[bank][:, 0:n]
                        if ci < 2:
                            act(gu[:, ci, 0:n], src, AF.Gelu_apprx_tanh, [pk(bank)], ["gu"])
                        elif ci < 4:
                            act(yT[:, ci - 2, t0:t0 + n], src, AF.Silu, [pk(bank)], [("yTa", t) for t in tiles])
                        elif ci >= 9:
                            act(yT[:, 2 + ci - 9, t0:t0 + n], src, AF.Silu, [pk(bank)],
                                [("yTb", t, kv) for t in tiles for kv in (0, 1)])
                        else:
                            isq = ci < 8
                            gcol = gqk[:, l, 0:1] if isq else gqk[:, l, 1:2]
                            dst = qT[:, ci - 4, t0:t0 + n] if isq else None
                            dkeys = [("qT", t) for t in tiles] if isq else [("kT", t) for t in tiles]
                            halves = [(slice(0, 128), dst)] if isq else [(slice(0, 64), kT[0:64, 0, t0:t0 + n]), (slice(64, 128), kT[64:128, 1, t0:t0 + n])]
                            act(sqq[:, 0:n], src, AF.Square, [pk(bank)], ["sqq"])
                            act(qnb[:, 0:n], src, AF.Identity, [pk(bank), "gqk"], ["qnb"], scale=gcol)
                            mm(PS[4][:, 0:n], bones[:], sqq[:, 0:n], True, True, ["sqq", "bones"], [pk(4)])
                            act(rs_t[:, 0:n], PS[4][:, 0:n], AF.Ln, [pk(4), "cst"], ["rs_t"], bias=eps_ap(), scale=1.0 / 64)
                            act(rq[:, 0:n], rs_t[:, 0:n], AF.Exp, ["rs_t"], ["rq"], scale=-0.5)
                            if not lat:
                                for hs_, d_ in halves:
                                    tt(d_, qnb[hs_, 0:n], rq[hs_, 0:n], ALU.mult, ["qnb", "rq"], dkeys)
                            else:
                                mm(PS[6][:, 0:n], rotb[:], qnb[:, 0:n], True, True, ["qnb", "rotb"], [pk(6)])
                                tt(t1[:, 0:n], qnb[:, 0:n], cosT[:, s0:s0 + n], ALU.mult, ["qnb", "cosT"], ["t1"])
                                tt(t2[:, 0:n], PS[6][:, 0:n], sinT[:, s0:s0 + n], ALU.mult, [pk(6), "sinT"], ["t2"])
                                tt(t1[:, 0:n], t1[:, 0:n], t2[:, 0:n], ALU.add, ["t1", "t2"], ["t1"])
                                for hs_, d_ in halves:
                                    tt(d_, t1[hs_, 0:n], rq[hs_, 0:n], ALU.mult, ["t1", "rq"], dkeys)
                    if bi + 1 < nblk:
                        norm_block(b, l, bi + 1, nxt=(bi + 2) % nblk, slot=(bi + 1) % 2)
                    ng = nt_ * 4
                    vgf = vg[:, 0:nt_, :].rearrange("p a (g d) -> p (a g) d", d=64)
                    vsf = vsq4[:, 0:nt_, :].rearrange("p a (g d) -> p (a g) d", d=64)
                    reduce(st4[:, 0:ng], vgf, ALU.add, ["vg"], ["st4"])
                    tt(vsf, vgf, vgf, ALU.mult, ["vg"], ["vsq4"])
                    reduce(st4[:, 16:16 + ng], vsf, ALU.add, ["vsq4"], ["st4"])
                    ts(st4[:, 0:ng], st4[:, 0:ng], 1.0 / 64, None, ALU.mult, None, ["st4"], ["st4"])
                    tt(st4[:, 32:32 + ng], st4[:, 0:ng], st4[:, 0:ng], ALU.mult, ["st4"], ["st4"])
                    stt(st4[:, 16:16 + ng], st4[:, 16:16 + ng], 1.0 / 64, st4[:, 32:32 + ng], ALU.mult, ALU.subtract, ["st4"], ["st4"])
                    act(st4[:, 32:32 + ng], st4[:, 16:16 + ng], AF.Ln, ["st4", "cst"], ["st4"], bias=eps_ap(), scale=1.0)
                    act(st4[:, 48:48 + ng], st4[:, 32:32 + ng], AF.Exp, ["st4"], ["st4"], scale=-0.5)
                    tt(vsf, vgf, st4[:, 0:ng].unsqueeze(2).to_broadcast([128, ng, 64]), ALU.subtract, ["vg", "st4"], ["vsq4"])
                    tt(vn[:, 0:ng, :], vsf, st4[:, 48:48 + ng].unsqueeze(2).to_broadcast([128, ng, 64]), ALU.mult,
                       ["vsq4", "st4"], ["vn"])
                    fbanks = [0, 1, 5]
                    for ci in [4, 5, 6, 7, 8]:
                        bank = fbanks[fcnt[0] % 3]
                        fcnt[0] += 1
                        for k in range(KC):
                            mm(PS[bank][:, 0:n], Wreg[:, k, ci * 128:(ci + 1) * 128], xnb[:, k, 0:n], k == 0, k == KC - 1,
                               [wkey(ci * 128), xnk], [pk(bank)])
                        src = PS[bank][:, 0:n]
                        if ci < 2:
                            act(gu[:, ci, 0:n], src, AF.Gelu_apprx_tanh, [pk(bank)], ["gu"])
                        elif ci < 4:
                            act(yT[:, ci - 2, t0:t0 + n], src, AF.Silu, [pk(bank)], [("yTa", t) for t in tiles])
                        elif ci >= 9:
                            act(yT[:, 2 + ci - 9, t0:t0 + n], src, AF.Silu, [pk(bank)],
                                [("yTb", t, kv) for t in tiles for kv in (0, 1)])
                        else:
                            isq = ci < 8
                            gcol = gqk[:, l, 0:1] if isq else gqk[:, l, 1:2]
                            dst = qT[:, ci - 4, t0:t0 + n] if isq else None
                            dkeys = [("qT", t) for t in tiles] if isq else [("kT", t) for t in tiles]
                            halves = [(slice(0, 128), dst)] if isq else [(slice(0, 64), kT[0:64, 0, t0:t0 + n]), (slice(64, 128), kT[64:128, 1, t0:t0 + n])]
                            act(sqq[:, 0:n], src, AF.Square, [pk(bank)], ["sqq"])
                            act(qnb[:, 0:n], src, AF.Identity, [pk(bank), "gqk"], ["qnb"], scale=gcol)
                            mm(PS[4][:, 0:n], bones[:], sqq[:, 0:n], True, True, ["sqq", "bones"], [pk(4)])
                            act(rs_t[:, 0:n], PS[4][:, 0:n], AF.Ln, [pk(4), "cst"], ["rs_t"], bias=eps_ap(), scale=1.0 / 64)
                            act(rq[:, 0:n], rs_t[:, 0:n], AF.Exp, ["rs_t"], ["rq"], scale=-0.5)
                            if not lat:
                                for hs_, d_ in halves:
                                    tt(d_, qnb[hs_, 0:n], rq[hs_, 0:n], ALU.mult, ["qnb", "rq"], dkeys)
                            else:
                                mm(PS[6][:, 0:n], rotb[:], qnb[:, 0:n], True, True, ["qnb", "rotb"], [pk(6)])
                                tt(t1[:, 0:n], qnb[:, 0:n], cosT[:, s0:s0 + n], ALU.mult, ["qnb", "cosT"], ["t1"])
                                tt(t2[:, 0:n], PS[6][:, 0:n], sinT[:, s0:s0 + n], ALU.mult, [pk(6), "sinT"], ["t2"])
                                tt(t1[:, 0:n], t1[:, 0:n], t2[:, 0:n], ALU.add, ["t1", "t2"], ["t1"])
                                for hs_, d_ in halves:
                                    tt(d_, t1[hs_, 0:n], rq[hs_, 0:n], ALU.mult, ["t1", "rq"], dkeys)
                    for ti, tg in enumerate(tiles):
                        c0 = ti * 128
                        gb_ = 2 + (ti % 2)
                        for g in range(4):
                            hp = 64 * (g % 2)
                            mm(PS[gb_][hp:hp + 64, (g // 2) * 128:(g // 2 + 1) * 128], vn[:, ti * 4 + g, :], wsT[:, g, :], True, True,
                               ["vn", "wsT"], [pk(gb_)])
                        tt(gtmp[:], PS[gb_][:, 0:256], bsT[:], ALU.add, [pk(gb_), "bsT"], ["gtmp"])
                        g3 = gtmp[:].rearrange("p (j q) -> p j q", j=2)
                        tt(g3, g3, gu[:, :, c0:c0 + 128], ALU.mult, ["gtmp", "gu"], ["gtmp"])
                        tt(yT[:, 0:2, tg * 128:(tg + 1) * 128], g3, yT[:, 0:2, tg * 128:(tg + 1) * 128], ALU.mult,
                           ["gtmp", ("yTa", tg)], [("yTa", tg)])
                dbg("qT", qT[:], [128, 4, T], [("qT", t) for t in range(NT)])
                dbg("kT", kT[:], [128, 2, T], [("kT", t) for t in range(NT)])
                dbg("yT_p1", yT[:], [128, 8, T], [("yTa", t) for t in range(NT)] + [("yTb", t, kv) for t in range(NT) for kv in (0, 1)])

                chk('p1')
                SC.barrier()
                norm_block(b, l, 0, nxt=1 % nblk, slot=0)
                load_w_cols(0, 2048, 512, l, "W")
                load_w_cols(512, 2304, 512, l, "W")
                load_w_cols(1024, 2816, 512, l, "W")
                load_w_cols(1536, 3328, 16, l, "W")
                its = []
                g_id = 0
                lat_q = list(range(NCX, NT))
                ctx_q = list(range(NCX))
                qorder = []
                while lat_q or ctx_q:
                    qorder += lat_q[:2]
                    lat_q = lat_q[2:]
                    if ctx_q:
                        qorder.append(ctx_q.pop(0))
                for kv in (0, 1):
                    for qb in qorder:
                        srange = list(range(NCX)) if qb < NCX else list(range(NT))
                        for i, sc in enumerate(srange):
                            its.append((kv, qb, i, sc, len(srange), 4 + (g_id % 3), g_id))
                        g_id += 1
                pending = []

                def epi_a(kv, qb, ob, gid):
                    o0 = 64 * kv
                    d0 = 64 * (1 - kv)
                    rd = rden[gid % 2]
                    recip(rd[d0:d0 + 64, :], PS[ob][d0:d0 + 64, :], [pk(ob)], [("rd", gid % 2, d0)])
                    dma("sp", rd[o0:o0 + 64, :], rd[d0:d0 + 64, :], "rdsw", reads=[("rd", gid % 2, d0)], writes=[("rd", gid % 2, o0)])

                def epi_b(kv, qb, ob, gid):
                    o0 = 64 * kv
                    rd = rden[gid % 2]
                    ysl = yT[o0:o0 + 64, 2:6, qb * 128:(qb + 1) * 128]
                    to3 = tmpo[o0:o0 + 64, :].rearrange("p (g q) -> p g q", g=4)
                    tt(to3, PS[ob][o0:o0 + 64, :].rearrange("p (g q) -> p g q", g=4), ysl, ALU.mult,
                       [pk(ob), ("yTb", qb, kv)], ["tmpo"])
                    tt(ysl, to3, rd[o0:o0 + 64, :].rearrange("p (g q) -> p g q", g=4), ALU.mult,
                       ["tmpo", ("rd", gid % 2, o0)], [("yTb", qb, kv)])

                NPT = len(pT)
                nit = len(its)

                def qk_pair(js):
                    args = []
                    rk, wk_ = [], []
                    for j in js:
                        kv, qb, i, sc, ns, ob, gid = its[j]
                        args.append((PS[j % 4][:].rearrange("p (g q) -> p g q", g=4), kT[:, kv, sc * 128:(sc + 1) * 128],
                                     qT[:, :, qb * 128:(qb + 1) * 128]))
                        rk += [("kT", sc), ("qT", qb)]
                        wk_.append(pk(j % 4))

                    def fn(h, args=args):
                        ins = None
                        for o_, l_, r_ in args:
                            ins = h.matmul(o_, lhsT=l_, rhs=r_, start=True, stop=True)
                        return ins
                    SC.op("pe", fn, reads=rk, writes=wk_)
                    if len(js) == 2:
                        b0, p0 = js[0] % 4, js[0] % NPT
                        act(pTall[:, p0:p0 + 2, :].rearrange("p a n -> p (a n)"), PSALL[:, b0 * 512:(b0 + 2) * 512], AF.Exp,
                            [pk(b0), pk(b0 + 1)], [("pT", p0), ("pT", p0 + 1)], scale=0.125)
                    else:
                        for j in js:
                            act(pT[j % NPT], PS[j % 4], AF.Exp, [pk(j % 4)], [("pT", j % NPT)], scale=0.125)

                def pv_pair(js):
                    args = []
                    rk, wk_ = [], []
                    for jj in js:
                        kv, qb, i, sc, ns, ob, gid = its[jj]
                        if i == 0:
                            for p in [p for p in pending if p[1][2] == ob]:
                                pending.remove(p)
                                epi_b(*p[1])
                        args.append((PS[ob][:], vB[:, sc, kv, :], pT[jj % NPT], i == 0, i == ns - 1))
                        rk += [("vB", sc), ("pT", jj % NPT)]
                        if pk(ob) not in wk_:
                            wk_.append(pk(ob))

                    def fn(h, args=args):
                        ins = None
                        for o_, l_, r_, st_, sp_ in args:
                            ins = h.matmul(o_, lhsT=l_, rhs=r_, start=st_, stop=sp_)
                        return ins
                    SC.op("pe", fn, reads=rk, writes=wk_)
                    for jj in js:
                        kv, qb, i, sc, ns, ob, gid = its[jj]
                        if i == ns - 1:
                            for p in [p for p in pending if p[1][3] % 2 == gid % 2]:
                                pending.remove(p)
                                epi_b(*p[1])
                            epi_a(kv, qb, ob, gid)
                            pending.append([3, (kv, qb, ob, gid)])

                for j in range(0, nit + 4, 2):
                    js = [x for x in (j, j + 1) if x < nit]
                    if js:
                        qk_pair(js)
                    pj = [x for x in (j - 4, j - 3) if 0 <= x < nit]
                    if pj:
                        pv_pair(pj)
                    for p in pending:
                        p[0] -= 1
                    while pending and pending[0][0] <= 0:
                        epi_b(*pending.pop(0)[1])
                while pending:
                    epi_b(*pending.pop(0)[1])
                dbg("yT_b", yT[:], [128, 8, T], [("yTa", t) for t in range(NT)] + [("yTb", t, kv) for t in range(NT) for kv in (0, 1)])
                SC.barrier()
                A.release(um)

                chk('b')
                qc = A("qc", [128, 4, T], BF16)
                kc = A("kc", [128, 2, T], BF16)
                ktok = A("ktok", [128, NT, 256], BF16)
                vC = A("vC", [128, NT, 4, 80], BF16)
                og = A("og", [128, NT, 256], BF16)
                hsum = A("hsum", [128, NT, 256], F32)
                gts = A("gts", [128, NT, 16], F32)
                sg = A("sg", [128, 512], F32)
                ogt = A("ogt", [128, 256], F32)
                G8 = [NT * 4]
                shp = [128, 2, NT, 4]
                SP = A("SP", shp, F32)
                TOT = A("TOT", shp, F32)
                BOF = A("BOF", shp, F32)
                BP = A("BP", shp, F32)
                GG = A("GG", shp, F32)
                CMR = A("CMR", shp, F32)
                RR = A("RR", shp, F32)
                RP = A("RP", shp, F32)
                WK = A("WK", shp, F32)
                FL = A("FL", shp, F32)
                LAM = A("LAM", shp, F32)
                TMPG = A("TMPG", shp, F32)
                LSEL = A("LSEL", [128, 2, NT, 2], F32)
                cm = A("cm", [128, 2], F32)
                Dg = A("Dg", [128, 128], F32)
                Sm = [A(f"Sm{i}", [128, 4, 128], BF16) for i in range(2)]
                vt = [A(f"vt{i}", [128, 4, 80], BF16) for i in range(2)]
                Cst = [A(f"Cst{i}", [128, 2, 80], F32) for i in range(2)]
                Cbf = [A(f"Cbf{i}", [128, 2, 80], BF16) for i in range(2)]
                dmx = [A(f"dmx{i}", [128, 8], F32) for i in range(2)]
                tmpx = [A(f"tmpx{i}", [128, 4, 64], F32) for i in range(2)]
                hq = A("hq", [128, 2, 256], F32)
                hst = A("hst", [128, 16], F32)
                yct = A("yct", [128, 2, 256], BF16)
                bgt = A("bgt", [128, 16], F32)
                ghd = A("ghd", [128, 256], F32)
                xb1 = None

                dma("sp", bgt[:], bg_d[l], "c0", writes=["bgt"])
                dma("sp", ghd[:], gh_d[l], "c0", writes=["ghd"])
                memset(vC[:], 1.0, [("vC", t) for t in range(NT)])
                memset(qc[:], 0.0, [("qc", t) for t in range(NT)])
                chk('p2a')

                for bi in range(nblk):
                    t0, n = cfg.blocks[bi]
                    r = NB if t0 < CTX else b
                    xnb = xnbuf[bi % 2]
                    xnk = ("xn", bi % 2)
                    tiles = [t0 // 128 + i for i in range(n // 128)]
                    for ci in range(4):
                        bank = ci % 2
                        for k in range(KC):
                            mm(PS[bank][:, 0:n], Wreg[:, k, ci * 128:(ci + 1) * 128], xnb[:, k, 0:n], k == 0, k == KC - 1,
                               ["Wlo_a", xnk], [pk(bank)])
                        if ci < 2:
                            act(qc[0:64, 2 * ci, t0:t0 + n], PS[bank][0:64, 0:n], AF.Identity, [pk(bank)], [("qc", t) for t in tiles], scale=0.125)
                            act(qc[64:128, 2 * ci + 1, t0:t0 + n], PS[bank][64:128, 0:n], AF.Identity, [pk(bank)], [("qc", t) for t in tiles], scale=0.125)
                        else:
                            vcopy(kc[:, ci - 2, t0:t0 + n], PS[bank][:, 0:n], [pk(bank)], [("kc", t) for t in tiles])
                    chk('p2b')
                    if bi + 1 < nblk:
                        norm_block(b, l, bi + 1, nxt=(bi + 2) % nblk, slot=(bi + 1) % 2)
                    for ti, tg in enumerate(tiles):
                        c0 = ti * 128
                        for k in range(KC):
                            mm(PS[2][:, 0:512], xnb[:, k, c0:c0 + 128], Wreg[:, k, 512:1024], k == 0, k == KC - 1, ["Wlo_q", xnk], [pk(2)])
                        for k in range(KC):
                            mm(PS[3][:, 0:512], xnb[:, k, c0:c0 + 128], Wreg[:, k, 1024:1536], k == 0, k == KC - 1, ["Whi", xnk], [pk(3)])
                        for k in range(KC):
                            mm(PS[4][:, 0:16], xnb[:, k, c0:c0 + 128], Wreg[:, k, 1536:1552], k == 0, k == KC - 1, ["Whi", xnk], [pk(4)])
                        chk('p2c')
                        SC.op("act", lambda h, o=ktok[:, tg, :], i=PS[2][:, 0:256]: h.copy(out=o, in_=i), reads=[pk(2)], writes=[("kt", tg)])
                        chk('p2d')
                        vcopy(vC[:, tg, :, 0:64], PS[2][:, 256:512].rearrange("p (h d) -> p h d", h=4), [pk(2)], [("vC", tg)])
                        chk('p2e')
                        act(sg[:], PS[3][:], AF.Sigmoid, [pk(3)], ["sg"])
                        chk('p2f')
                        tt(ogt[:], sg[:, 0:256], sg[:, 256:512], ALU.mult, ["sg"], ["ogt"])
                        tt(og[:, tg, :], ogt[:], PS[3][:, 256:512], ALU.mult, ["ogt", pk(3)], [("og", tg)])
                        chk('p2g')
                        vcopy(gts[:, tg, :], PS[4][:, 0:16], [pk(4)], ["gts"])
                        chk('p2h')

                defer_mod = (b == 0 and l + 1 < DEPTH)
                if defer_mod:
                    mod_dma(l + 1, 1024, ["Whi"], 0)
                dma("pool", Wreg[:, 0:2, 0:1024], wout_d[l, 0:256, :].rearrange("(k p) n -> p k n", p=128), "W", writes=["Wlo_a", "Wlo_q"])
                for j in range(4):
                    dma("pool", Wreg[0:64, 2 + j, 0:1024], wout_d[l, 256 + 64 * j:256 + 64 * j + 64, :], "W", writes=["Wlo_a", "Wlo_q"])
                    dma("pool", Wreg[64:128, 2 + j, 0:1024], wout_d[l, 256 + 64 * (4 + j):256 + 64 * (4 + j) + 64, :], "W", writes=["Wlo_a", "Wlo_q"])
                dma("pool", Wreg[:, 6:8, 0:1024], wout_d[l, 768:1024, :].rearrange("(k p) n -> p k n", p=128), "W", writes=["Wlo_a", "Wlo_q"])
                chk('p2')
                ordf = list(range(NT))
                ordb = list(range(NCX - 1, -1, -1)) + list(range(NT - 1, NCX - 1, -1))
                orders = [ordf, ordb]
                tt(gts[:], gts[:], bgt[:].unsqueeze(1).to_broadcast([128, NT, 16]), ALU.add, ["gts", "bgt"], ["gts"])
                for dr in range(2):
                    act(SP[:, dr, :, :], gts[:, :, 4 + 8 * dr:8 + 8 * dr], AF.Exp, ["gts"], ["SP"], scale=-1.0)
                act(SP[:], SP[:], AF.Ln, ["SP", "cst"], ["SP"], bias=one_ap(), scale=1.0)
                n4 = NT * 4
                for dr in range(2):
                    mm(PS[0][:, dr * n4:(dr + 1) * n4], (trif if dr == 0 else trib)[:], SP[:, dr, :, :].rearrange("p c h -> p (c h)"),
                       True, True, ["SP", "trif", "trib"], [pk(0)])
                mm(PS[1][:, 0:2 * n4], ones_f[:], SP[:].rearrange("p d c h -> p (d c h)"), True, True, ["SP", "ones_f"], [pk(1)])
                vcopy(TOT[:].rearrange("p d c h -> p (d c h)"), PS[1][:, 0:2 * n4], [pk(1)], ["TOT"])
                memset(BOF[:], 0.0, ["BOF"])
                for dr in range(2):
                    od = orders[dr]
                    for i in range(1, NT):
                        tt(BOF[:, dr, od[i], :], BOF[:, dr, od[i - 1], :], TOT[:, dr, od[i - 1], :], ALU.add, ["BOF", "TOT"], ["BOF"])
                tt(BP[:].rearrange("p d c h -> p (d c h)"), PS[0][:, 0:2 * n4], BOF[:].rearrange("p d c h -> p (d c h)"), ALU.add,
                   [pk(0), "BOF"], ["BP"])
                for dr in range(2):
                    tt(GG[:, dr, :, :], BP[:, dr, :, :], gts[:, :, 8 * dr:8 * dr + 4], ALU.add, ["BP", "gts"], ["GG"])
                for dr in range(2):
                    tr(PS[5][0:n4, dr * 128:(dr + 1) * 128], GG[:, dr, :, :].rearrange("p c h -> p (c h)"), ident[:], ["GG", "ident"], [pk(5)])
                for dr in range(2):
                    SC.op("dve", lambda h, o=cm[0:n4, dr:dr + 1], i=PS[5][0:n4, dr * 128:(dr + 1) * 128]:
                          h.tensor_reduce(out=o, in_=i, axis=AX.X, op=ALU.max), reads=[pk(5)], writes=["cm"])
                for dr in range(2):
                    ts(Dg[0:n4, 0:n4], ident[0:n4, 0:n4], cm[0:n4, dr:dr + 1], None, ALU.mult, None, ["cm", "ident"], ["Dg"])
                    mm(PS[0][:, 256 + dr * n4:256 + (dr + 1) * n4], ones_f[0:n4, :], Dg[0:n4, 0:n4], True, True, ["Dg", "ones_f"], [pk(0)])
                vcopy(CMR[:].rearrange("p d c h -> p (d c h)"), PS[0][:, 256:256 + 2 * n4], [pk(0)], ["CMR"])
                for dr in range(2):
                    od = orders[dr]
                    prev = zero4[:, 0:4]
                    for c in od:
                        vcopy(RP[:, dr, c, :], prev, ["RR", "zero4"], ["RP"])
                        tt(RR[:, dr, c, :], prev, CMR[:, dr, c, :], ALU.max, ["RR", "CMR", "zero4"], ["RR"])
                        prev = RR[:, dr, c, :]
                tt(TMPG[:], GG[:], RR[:], ALU.subtract, ["GG", "RR"], ["TMPG"])
                act(WK[:], TMPG[:], AF.Exp, ["TMPG"], ["WK"])
                tt(TMPG[:], BP[:], RR[:], ALU.subtract, ["BP", "RR"], ["TMPG"])
                act(FL[:], TMPG[:], AF.Exp, ["TMPG"], ["FL"])
                tt(TMPG[:], RP[:], RR[:], ALU.subtract, ["RP", "RR"], ["TMPG"])
                act(LAM[:], TMPG[:], AF.Exp, ["TMPG"], ["LAM"])
                vcopy(LSEL[0:64, :, :, :], LAM[0:64, :, :, 0:4:2], ["LAM"], ["LSEL"])
                vcopy(LSEL[64:128, :, :, :], LAM[64:128, :, :, 1:4:2], ["LAM"], ["LSEL"])
                dbg("WK", WK[:], shp, ["WK"])
                dbg("FL", FL[:], shp, ["FL"])
                dbg("LAM", LAM[:], shp, ["LAM"])

                chk('c1')
                touched = set()
                mod_steps = {max(1, NT // 4): 0, max(2, NT // 2): 1, max(3, (3 * NT) // 4): 2}
                for i in range(NT):
                    if defer_mod and i in mod_steps:
                        m3_ = mod_steps[i]
                        mod_mm(l + 1, 1024, ["Whi"], 6, m3_)
                        if m3_ < 2:
                            mod_dma(l + 1, 1024, ["Whi"], m3_ + 1)
                        else:
                            mod_fin(l + 1, 6)
                    for dr in range(2):
                        c = orders[dr][i]
                        cs = slice(c * 128, (c + 1) * 128)
                        bS, bX, bD = (0, 1, 2) if dr == 0 else (3, 4, 5)
                        for hh in range(4):
                            hp = 64 * (hh % 2)
                            mm(PS[bS][:, hh * 128:(hh + 1) * 128], kc[:, hh // 2, cs], qc[:, hh, cs], True, True,
                               [("kc", c), ("qc", c)], [pk(bS)])
                        tt(Sm[dr][:], PS[bS][:].rearrange("p (h t) -> p h t", h=4),
                           (maskf if dr == 0 else maskb)[:], ALU.mult,
                           [pk(bS), "maskf", "maskb"], [("Sm", dr)])
                        chk(f"L{i}{dr}a")
                        tt(vt[dr][:, :, 0:65], vC[:, c, :, 0:65], WK[:, dr, c, :].unsqueeze(2).to_broadcast([128, 4, 65]), ALU.mult,
                           [("vC", c), "WK"], [("vt", dr)])
                        chk(f"L{i}{dr}b")
                        if i > 0:
                            tt(Cst[dr][:, :, 0:65], Cst[dr][:, :, 0:65], LSEL[:, dr, c, :].unsqueeze(2).to_broadcast([128, 2, 65]), ALU.mult,
                               [("Cst", dr), "LSEL"], [("Cst", dr)])
                            SC.op("act", lambda h, o=Cbf[dr][:, :, 0:65], i_=Cst[dr][:, :, 0:65]: h.copy(out=o, in_=i_),
                                  reads=[("Cst", dr)], writes=[("Cbf", dr)])
                        for hh in range(4):
                            hp = 64 * (hh % 2)
                            mm(PS[bX][:, hh * 80:hh * 80 + 65], Sm[dr][:, hh, :], vt[dr][:, hh, 0:65], True, i == 0,
                               [("Sm", dr), ("vt", dr)], [pk(bX)])
                            if i > 0:
                                mm(PS[bX][:, hh * 80:hh * 80 + 65], qc[:, hh, cs], Cbf[dr][:, hh // 2, 0:65],
                                   False, True, [("qc", c), ("Cbf", dr)], [pk(bX)])
                        chk(f"L{i}{dr}c")
                        for hh in range(4):
                            hp = 64 * (hh % 2)
                            mm(PS[bD][hp:hp + 64, (hh // 2) * 80:(hh // 2) * 80 + 65], ktok[:, c, hh * 64:(hh + 1) * 64],
                               vt[dr][:, hh, 0:65], True, True, [("kt", c), ("vt", dr)], [pk(bD)])
                        chk(f"L{i}{dr}d")
                        pd = PS[bD][:, 0:160].rearrange("p (j e) -> p j e", j=2)[:, :, 0:65]
                        if i == 0:
                            vcopy(Cst[dr][:, :, 0:65], pd, [pk(bD)], [("Cst", dr)])
                        else:
                            tt(Cst[dr][:, :, 0:65], Cst[dr][:, :, 0:65], pd, ALU.add, [("Cst", dr), pk(bD)], [("Cst", dr)])
                        chk(f"L{i}{dr}e")
                        px = PS[bX][:, 0:320].rearrange("p (h e) -> p h e", h=4)
                        stt(dmx[dr][:, 0:4], px[:, :, 64], -1.0, FL[:, dr, c, :], ALU.mult, ALU.max, [pk(bX), "FL"], [("dmx", dr)])
                        tt(dmx[dr][:, 0:4], dmx[dr][:, 0:4], px[:, :, 64], ALU.max, [pk(bX), ("dmx", dr)], [("dmx", dr)])
                        recip(dmx[dr][:, 4:8], dmx[dr][:, 0:4], [("dmx", dr)], [("dmx", dr)])
                        h3 = hsum[:, c, :].rearrange("p (h d) -> p h d", h=4)
                        if c not in touched:
                            touched.add(c)
                            tt(h3, px[:, :, 0:64], dmx[dr][:, 4:8].unsqueeze(2).to_broadcast([128, 4, 64]), ALU.mult,
                               [pk(bX), ("dmx", dr)], [("hs", c)])
                        else:
                            tt(tmpx[dr][:], px[:, :, 0:64], dmx[dr][:, 4:8].unsqueeze(2).to_broadcast([128, 4, 64]), ALU.mult,
                               [pk(bX), ("dmx", dr)], [("tmpx", dr)])
                            tt(h3, h3, tmpx[dr][:], ALU.add, [("hs", c), ("tmpx", dr)], [("hs", c)])
                dbg("hsum", hsum[:], [128, NT, 256], [("hs", c) for c in range(NT)])
                chk('c2')
                ui = units.index((b, l))
                if ui + 1 < len(units):
                    load_w1(units[ui + 1][1], "hi")
                    w1hi_ready["u"] = units[ui + 1]
                for c0_ in range(0, NT, 2):
                    cs_ = [c for c in (c0_, c0_ + 1) if c < NT]
                    m_ = len(cs_)
                    hc2 = hsum[:, c0_:c0_ + m_, :]
                    hkeys = [("hs", c) for c in cs_]
                    tt(hq[:, 0:m_, :], hc2, hc2, ALU.mult, hkeys, ["hq"])
                    reduce(hst[:, 0:4 * m_], hq[:, 0:m_, :].rearrange("p a (h d) -> p (a h) d", h=4), ALU.add, ["hq"], ["hst"])
                    act(hst[:, 8:8 + 4 * m_], hst[:, 0:4 * m_], AF.Ln, ["hst", "cst"], ["hst"], bias=eps_ap(), scale=1.0 / 64)
                    act(hst[:, 0:4 * m_], hst[:, 8:8 + 4 * m_], AF.Exp, ["hst"], ["hst"], scale=-0.5)
                    tt(hq[:, 0:m_, :].rearrange("p a (h d) -> p (a h) d", h=4), hc2.rearrange("p a (h d) -> p (a h) d", h=4),
                       hst[:, 0:4 * m_].unsqueeze(2).to_broadcast([128, 4 * m_, 64]), ALU.mult, hkeys + ["hst"], ["hq"])
                    for a_, c in enumerate(cs_):
                        tt(hq[:, a_, :], hq[:, a_, :], ghd[:], ALU.mult, ["hq", "ghd"], ["hq"])
                    tt(yct[:, 0:m_, :], hq[:, 0:m_, :], og[:, c0_:c0_ + m_, :], ALU.mult, ["hq"] + [("og", c) for c in cs_], ["yct"])
                    for a_, c in enumerate(cs_):
                        for j in range(2):
                            tr(PSB[:, j * 512 + (c % 4) * 128:j * 512 + (c % 4 + 1) * 128], yct[:, a_, j * 128:(j + 1) * 128], identb[:],
                               ["yct", "identb"], ["psbf"])
                        if c % 4 == 3 or c == NT - 1:
                            cb = (c // 4) * 4
                            w = (c - cb + 1) * 128
                            SC.op("act", lambda h, o=yT[:, 6:8, cb * 128:cb * 128 + w], i_=PSB[:].rearrange("p (j t) -> p j t", j=2)[:, :, 0:w]:
                                  h.copy(out=o, in_=i_), reads=["psbf"], writes=[("yTc", t) for t in range(cb, c + 1)])
                dbg("yT_c", yT[:], [128, 8, T], [("yTc", t) for t in range(NT)])
                SC.barrier()
                A.release(um)

                chk('c')
                xb1 = A("xb1", [128, KC, 512], F32)
                allkeys = ([("yTa", t) for t in range(NT)] + [("yTb", t, kv) for t in range(NT) for kv in (0, 1)]
                           + [("yTc", t) for t in range(NT)])
                def oload(bi_):
                    t0_, n_ = cfg.blocks[bi_]
                    if bi_ % 2 == 0:
                        xload(b, bi_)
                        xstate["loaded"] = None
                    else:
                        dma("pool", xb1[:, :, 0:n_], xT_d[b, :, t0_:t0_ + n_].rearrange("(k p) t -> p k t", p=128), "xldo",
                            reads=[("xd", b, bi_)], writes=["xb1"])

                oload(0)
                ocnt_ = 0
                for bi, (t0, n) in enumerate(cfg.blocks):
                    r = NB if t0 < CTX else b
                    xb = xb0 if bi % 2 == 0 else xb1
                    xk = "xb0" if bi % 2 == 0 else "xb1"
                    tiles = [t0 // 128 + i for i in range(n // 128)]
                    ykeys = ([("yTa", t) for t in tiles] + [("yTb", t, kv) for t in tiles for kv in (0, 1)] + [("yTc", t) for t in tiles])
                    if bi + 1 < len(cfg.blocks):
                        oload(bi + 1)
                    for m in range(8):
                        bank = ocnt_ % 4
                        ocnt_ += 1
                        for k in range(KC):
                            mm(PS[bank][:, 0:n], Wreg[:, k, m * 128:(m + 1) * 128], yT[:, k, t0:t0 + n], k == 0, k == KC - 1,
                               ["Wlo_a", "Wlo_q"] + ykeys, [pk(bank)])
                        stt(xb[:, m, 0:n], PS[bank][:, 0:n], modt[:, l, 16 + m, r:r + 1], xb[:, m, 0:n], ALU.mult, ALU.add,
                            [pk(bank), ("modt", l), xk], [xk])
                    dma("sp", xT_d[b, :, t0:t0 + n].rearrange("(k p) t -> p k t", p=128), xb[:, :, 0:n], "xst",
                        reads=[xk], writes=[("xd", b, bi)])
                    if bi == 0 and l + 1 < DEPTH:
                        norm_block(b, l + 1, 0, nxt=None, slot=0, preloaded=True)
                        early["p1"] = (b, l + 1)
                SC.barrier()
                A.release(um)

            chk('o')
            fm = A.mark()
            xbf1 = A("xbf1", [128, KC, 512], F32)
            otile = [A(f"otile{i}", [128, D], F32) for i in range(3)]
            latb = [(bi, t0, n) for bi, (t0, n) in enumerate(cfg.blocks) if t0 >= CTX]
            xstate["loaded"] = None

            def fload(idx):
                bi_, t0_, n_ = latb[idx]
                buf_ = xb0 if idx % 2 == 0 else xbf1
                key_ = "xb0" if idx % 2 == 0 else "xbf1"
                dma("pool", buf_[:, :, 0:n_], xT_d[b, :, t0_:t0_ + n_].rearrange("(k p) t -> p k t", p=128), "xldf",
                    reads=[("xd", b, bi_)], writes=[key_])

            fload(0)
            oc = 0
            for idx, (bi, t0, n) in enumerate(latb):
                if idx + 1 < len(latb):
                    fload(idx + 1)
                xbf = xb0 if idx % 2 == 0 else xbf1
                xk = "xb0" if idx % 2 == 0 else "xbf1"
                for k in range(KC):
                    sk = ("sqk", k % 2)
                    act(sqk[k % 2][:, 0:n], xbf[:, k, 0:n], AF.Square, [xk], [sk])
                    mm(PS[6][:, 0:n], ones_b[:], sqk[k % 2][:, 0:n], k == 0, k == KC - 1, [sk, "ones_b"], [pk(6)])
                act(rs_t[:, 0:n], PS[6][:, 0:n], AF.Ln, [pk(6), "cst"], ["rs_t"], bias=eps_ap(), scale=1.0 / D)
                act(rstd[:, 0:n], rs_t[:, 0:n], AF.Exp, ["rs_t"], ["rstd"], scale=-0.5)
                for k in range(KC):
                    stt(xbf[:, k, 0:n], xbf[:, k, 0:n], gfin[:, k:k + 1], rstd[:, 0:n], ALU.mult, ALU.mult,
                        [xk, "gfin", "rstd"], [xk])
                for ti in range(n // 128):
                    ot = otile[oc % 3]
                    okey = ("otile", oc % 3)
                    bA, bB = (1, 2) if oc % 2 == 0 else (3, 4)
                    oc += 1
                    for k in range(KC):
                        bank = bA if k < 4 else bB
                        tr(PS[bank][:, (k % 4) * 128:(k % 4 + 1) * 128], xbf[:, k, ti * 128:(ti + 1) * 128], ident[:],
                           [xk, "ident"], [pk(bank)])
                    vcopy(ot[:, 0:512], PS[bA][:], [pk(bA)], [okey])
                    SC.op("act", lambda h, o=ot[:, 512:1024], i_=PS[bB][:]: h.copy(out=o, in_=i_), reads=[pk(bB)], writes=[okey])
                    s0 = t0 - CTX + ti * 128
                    dma("sp", out_d[b, s0:s0 + 128, :], ot[:], "ost", reads=[okey], writes=[("out", b, s0)])
            A.release(fm)
            SC.barrier()

    except _Stop:
        pass
    SC.barrier()
    nc._sbuf_peak = A.peak
    nc._nops = {e: len(SC.ops[e]) for e in ENGS}
    fns = SC.emit()
    with nc.Block() as block:
        block.sync(fns["sp"])
        block.scalar(fns["act"])
        block.vector(fns["dve"])
        block.gpsimd(fns["pool"])
        block.tensor(fns["pe"])
    return nc


def make_in_maps(cfg, n_cores, x, c, ctx, c_ctx, w_ada, b_ada, g_norm, w_in, w_s, b_s, g_q, g_k, b_gates, g_head,
                 w_out, g_final):
    f = lambda a: np.ascontiguousarray(np.asarray(a, dtype=np.float32))
    DEPTH, NB, NR = cfg.DEPTH, cfg.NB, cfg.NR
    consts = _consts(cfg)
    shared = dict(consts)
    shared["w_ada"] = f(w_ada)
    shared["w_in"] = f(w_in)
    shared["w_s"] = f(w_s)
    shared["w_out"] = f(w_out)
    shared["b_adaT"] = np.stack([_vecT(f(b_ada)[l], 24) for l in range(DEPTH)])
    shared["g_normT"] = np.stack([_vecT(f(g_norm)[l], KC) for l in range(DEPTH)])
    shared["g_finalT"] = _vecT(f(g_final), KC)
    bs = f(b_s)
    bsT = np.zeros((DEPTH, 128, 256), np.float32)
    for l in range(DEPTH):
        for j in range(2):
            for i in range(2):
                bsT[l, 64 * i:64 * i + 64, j * 128:(j + 1) * 128] = bs[l, 2 * j + i][None, :]
    shared["bsT"] = bsT
    shared["gq"] = np.ascontiguousarray(np.tile(f(g_q), (1, 2))[:, :, None])
    shared["gk"] = np.ascontiguousarray(np.tile(f(g_k), (1, 2))[:, :, None])
    shared["bgate"] = np.ascontiguousarray(np.broadcast_to(f(b_gates)[:, None, :], (DEPTH, 128, 16)))
    shared["ghead"] = np.ascontiguousarray(np.broadcast_to(f(g_head)[:, None, :], (DEPTH, 128, 256)))
    x = f(x)
    c = f(c)
    ctx = f(ctx)
    c_ctx = f(c_ctx)
    maps = []
    for i in range(n_cores):
        sl = slice(i * NB, (i + 1) * NB)
        rows = np.concatenate([c[sl], c_ctx[None, :]], axis=0)
        cT = np.ascontiguousarray(rows.reshape(NR, KC, 128).transpose(2, 1, 0).reshape(128, KC * NR))
        m = dict(shared)
        m["x"] = np.ascontiguousarray(x[sl])
        m["ctx"] = np.ascontiguousarray(ctx[sl])
        m["cT"] = cT
        maps.append(m)
    return maps


_NC_CACHE = {}


def kernel(x, c, ctx, c_ctx, w_ada, b_ada, g_norm, w_in, w_s, b_s, g_q, g_k, b_gates, g_head, w_out, g_final):
    n_cores = 8
    B, S, _ = np.asarray(x).shape
    CTX = np.asarray(ctx).shape[1]
    DEPTH = np.asarray(w_in).shape[0]
    cfg = Cfg(S=S, CTX=CTX, DEPTH=DEPTH, NB=B // n_cores)
    key = (S, CTX, DEPTH, cfg.NB)
    if key not in _NC_CACHE:
        _NC_CACHE[key] = build(cfg)
    nc = _NC_CACHE[key]
    maps = make_in_maps(cfg, n_cores, x, c, ctx, c_ctx, w_ada, b_ada, g_norm, w_in, w_s, b_s, g_q, g_k, b_gates,
                        g_head, w_out, g_final)
    res = run_bass_kernel_spmd(nc, maps, core_ids=list(range(n_cores)))
    return np.concatenate([r["out"] for r in res.results], axis=0).astype(np.float32)
```
